# Optimizing a Trainium2 kernel written in Bass

```python
import math
import jax, jax.numpy as jnp
from jax import lax
import numpy as np

D_MODEL = 1024
BATCH = 16
SEQ = 2048
DEPTH = 1
DEC_BATCH = 16
DEC_SEQ = 4096
PAST_LEN = 128

GRID_W = 64
N_Q_HEADS = 16
N_KV_HEADS = 4
Q_PER_KV = N_Q_HEADS // N_KV_HEADS
HEAD_DIM = 64
ATT_W = N_Q_HEADS * HEAD_DIM
KV_W = N_KV_HEADS * HEAD_DIM
AXIS_DIM = HEAD_DIM // 2
ROPE_THETA = 10000.0
Q_BLOCK = 128
SSM_EXPAND = 2
D_INNER = SSM_EXPAND * D_MODEL
SSM_HEAD_DIM = 64
N_SSM_HEADS = D_INNER // SSM_HEAD_DIM
N_SSM_GROUPS = 4
D_STATE = 128
CONV_W = 5
CHUNK = 128
CONV_CH = D_INNER + 2 * N_SSM_GROUPS * D_STATE
DT_MIN = 0.001
DT_MAX = 0.1
A_MIN = 1.0
A_MAX = 16.0
RMS_EPS = 1e-6
LN_EPS = 1e-5
ALPHA = (2 * DEPTH) ** 0.25
INIT_BETA = (8 * DEPTH) ** -0.25
SPLITS = (ATT_W, KV_W, KV_W, ATT_W, D_INNER, CONV_CH, 2 * N_SSM_HEADS, 2 * D_MODEL)
IN_W = sum(SPLITS)
SPLIT_POINTS = tuple(int(v) for v in np.cumsum(SPLITS)[:-1])

kernel_name = 'hybrid_gqa_ssd_encoder'


def _rms_norm(x, w):
    xf = x.astype(jnp.float32)
    xf = xf * lax.rsqrt(jnp.mean(xf * xf, axis=-1, keepdims=True) + RMS_EPS)
    return xf.astype(x.dtype) * w


def _layer_norm(x, g, b):
    xf = x.astype(jnp.float32)
    mu = jnp.mean(xf, axis=-1, keepdims=True)
    var = jnp.mean(jnp.square(xf - mu), axis=-1, keepdims=True)
    return ((xf - mu) * lax.rsqrt(var + LN_EPS)).astype(x.dtype) * g + b


def _axial_rope(seq_len):
    rows = seq_len // GRID_W
    row = jnp.repeat(jnp.arange(rows, dtype=jnp.float32), GRID_W)
    col = jnp.tile(jnp.arange(GRID_W, dtype=jnp.float32), rows)
    freqs = ROPE_THETA ** (-jnp.arange(0, AXIS_DIM, 2, dtype=jnp.float32) / AXIS_DIM)
    ang = jnp.concatenate([row[:, None] * freqs, col[:, None] * freqs], axis=-1)
    return jnp.cos(ang), jnp.sin(ang)


def _apply_rope(x, cos, sin):
    xf = x.astype(jnp.float32).reshape(*x.shape[:-1], HEAD_DIM // 2, 2)
    x0, x1 = xf[..., 0], xf[..., 1]
    c, s = cos[:, None, :], sin[:, None, :]
    out = jnp.stack([x0 * c - x1 * s, x0 * s + x1 * c], axis=-1)
    return out.reshape(x.shape).astype(x.dtype)


def _block_attention(q, k, v):
    b, S = q.shape[0], q.shape[1]
    nb = S // Q_BLOCK
    qb = q.reshape(b, nb, Q_BLOCK, N_KV_HEADS, Q_PER_KV, HEAD_DIM).transpose(1, 0, 2, 3, 4, 5)

    def one_block(qi):
        s = jnp.einsum('bqgrd,bkgd->bgrqk', qi, k).astype(jnp.float32)
        p = jax.nn.softmax(s, axis=-1).astype(v.dtype)
        return jnp.einsum('bgrqk,bkgd->bqgrd', p, v)

    o = lax.map(one_block, qb)
    return o.transpose(1, 0, 2, 3, 4, 5).reshape(b, S, ATT_W)


def _depthwise_conv(x, w, bias):
    c = x.shape[-1]
    y = lax.conv_general_dilated(x, w[:, None, :].astype(x.dtype), window_strides=(1,),
                                 padding=[(CONV_W // 2, CONV_W // 2)],
                                 dimension_numbers=('NWC', 'WIO', 'NWC'),
                                 feature_group_count=c)
    return y + bias


def _ssd_chunked(x, dt, a, bm, cm):
    b, L, H, P = x.shape
    G, N = bm.shape[2], bm.shape[3]
    R = H // G
    nc = L // CHUNK
    xg = x.astype(jnp.float32).reshape(b, nc, CHUNK, G, R, P)
    bc = bm.astype(jnp.float32).reshape(b, nc, CHUNK, G, N)
    cc = cm.astype(jnp.float32).reshape(b, nc, CHUNK, G, N)
    dtc = dt.reshape(b, nc, CHUNK, G, R)
    acs = jnp.cumsum(dtc * a.reshape(G, R), axis=2)
    lower = jnp.tril(jnp.ones((CHUNK, CHUNK), dtype=bool))
    seg = acs[:, :, :, None] - acs[:, :, None]
    lmat = jnp.exp(jnp.where(lower[:, :, None, None], seg, -jnp.inf))
    cb = jnp.einsum('bcign,bcjgn->bcijg', cc, bc)
    scores = cb[..., None] * lmat * dtc[:, :, None]
    y_diag = jnp.einsum('bcijgr,bcjgrp->bcigrp', scores, xg)
    xw = xg * (jnp.exp(acs[:, :, -1:] - acs) * dtc)[..., None]
    states = jnp.einsum('bcjgn,bcjgrp->bcgrpn', bc, xw)
    chunk_decay = jnp.exp(acs[:, :, -1])

    def step(h, inp):
        st, dec = inp
        return dec[..., None, None] * h + st, h

    h0 = jnp.zeros((b, G, R, P, N), jnp.float32)
    _, h_prev = lax.scan(step, h0, (jnp.moveaxis(states, 1, 0), jnp.moveaxis(chunk_decay, 1, 0)))
    h_prev = jnp.moveaxis(h_prev, 0, 1)
    y_off = jnp.einsum('bcign,bcgrpn->bcigrp', cc, h_prev) * jnp.exp(acs)[..., None]
    return (y_diag + y_off).reshape(b, L, H, P)


def _hybrid_layer(x, w_in, b_gate, q_norm_w, k_norm_w, conv_w, conv_b, dt_bias_fwd, dt_bias_bwd,
                  a_log_fwd, a_log_bwd, d_skip, ssm_norm_w, w_att_proj, w_ssm_proj, w_out, ln_g, ln_b):
    b, S, _ = x.shape
    proj = jnp.einsum('bsd,de->bse', x, w_in)
    q, k, v, g_att, z, xbc, dt_raw, gate_raw = jnp.split(proj, SPLIT_POINTS, axis=-1)

    q = _rms_norm(q.reshape(b, S, N_Q_HEADS, HEAD_DIM), q_norm_w)
    k = _rms_norm(k.reshape(b, S, N_KV_HEADS, HEAD_DIM), k_norm_w)
    v = v.reshape(b, S, N_KV_HEADS, HEAD_DIM)
    cos, sin = _axial_rope(S)
    q = _apply_rope(q, cos, sin) * (HEAD_DIM ** -0.5)
    k = _apply_rope(k, cos, sin)
    att = _block_attention(q, k, v) * jax.nn.silu(g_att)
    att = jnp.einsum('bse,ed->bsd', att, w_att_proj)

    xbc = jax.nn.silu(_depthwise_conv(xbc, conv_w, conv_b))
    xs, bm, cm = jnp.split(xbc, [D_INNER, D_INNER + N_SSM_GROUPS * D_STATE], axis=-1)
    xs = xs.reshape(b, S, N_SSM_HEADS, SSM_HEAD_DIM)
    bm = bm.reshape(b, S, N_SSM_GROUPS, D_STATE)
    cm = cm.reshape(b, S, N_SSM_GROUPS, D_STATE)
    dt_f, dt_b = jnp.split(dt_raw.astype(jnp.float32), 2, axis=-1)
    dt_f = jax.nn.softplus(dt_f + dt_bias_fwd.astype(jnp.float32))
    dt_b = jax.nn.softplus(dt_b + dt_bias_bwd.astype(jnp.float32))
    a_f = -jnp.exp(a_log_fwd.astype(jnp.float32))
    a_b = -jnp.exp(a_log_bwd.astype(jnp.float32))
    y_f = _ssd_chunked(xs, dt_f, a_f, bm, cm)
    flip = lambda t: jnp.flip(t, axis=1)
    y_b = flip(_ssd_chunked(flip(xs), flip(dt_b), a_b, flip(bm), flip(cm)))
    y = y_f + y_b + xs.astype(jnp.float32) * d_skip.astype(jnp.float32)[:, None]
    y = y.reshape(b, S, D_INNER).astype(x.dtype) * jax.nn.silu(z)
    y = _rms_norm(y.reshape(b, S, N_SSM_GROUPS, D_INNER // N_SSM_GROUPS),
                  ssm_norm_w.reshape(N_SSM_GROUPS, D_INNER // N_SSM_GROUPS)).reshape(b, S, D_INNER)
    ssm = jnp.einsum('bse,ed->bsd', y, w_ssm_proj)

    gates = jax.nn.sigmoid(gate_raw + b_gate)
    g_a, g_s = jnp.split(gates, 2, axis=-1)
    mixed = g_a * att + g_s * ssm
    out = jnp.einsum('bsd,de->bse', mixed, w_out)
    return _layer_norm(ALPHA * x + out, ln_g, ln_b)


def setup_inputs(seed: int = 0) -> dict:
    key = jax.random.key(seed)
    keys = jax.random.split(key, 20)
    f32 = jnp.float32

    def nrm(k, shape, scale):
        return jax.random.normal(k, shape, f32) * scale

    def dt_bias(k):
        dt = jnp.exp(jax.random.uniform(k, (DEPTH, N_SSM_HEADS), f32, math.log(DT_MIN), math.log(DT_MAX)))
        return dt + jnp.log(-jnp.expm1(-dt))

    return {
        'x_prompt': nrm(keys[0], (BATCH, SEQ, D_MODEL), 1.0),
        'x_sample': nrm(keys[1], (DEC_BATCH, DEC_SEQ, D_MODEL), 1.0),
        'w_in': nrm(keys[2], (DEPTH, D_MODEL, IN_W), D_MODEL ** -0.5),
        'b_gate': nrm(keys[3], (DEPTH, 2 * D_MODEL), 0.02),
        'q_norm_w': 1.0 + nrm(keys[4], (DEPTH, HEAD_DIM), 0.02),
        'k_norm_w': 1.0 + nrm(keys[5], (DEPTH, HEAD_DIM), 0.02),
        'conv_w': nrm(keys[6], (DEPTH, CONV_W, CONV_CH), CONV_W ** -0.5),
        'conv_b': nrm(keys[7], (DEPTH, CONV_CH), 0.02),
        'dt_bias_fwd': dt_bias(keys[8]),
        'dt_bias_bwd': dt_bias(keys[9]),
        'a_log_fwd': jnp.log(jax.random.uniform(keys[10], (DEPTH, N_SSM_HEADS), f32, A_MIN, A_MAX)),
        'a_log_bwd': jnp.log(jax.random.uniform(keys[11], (DEPTH, N_SSM_HEADS), f32, A_MIN, A_MAX)),
        'd_skip': 1.0 + nrm(keys[12], (DEPTH, N_SSM_HEADS), 0.1),
        'ssm_norm_w': 1.0 + nrm(keys[13], (DEPTH, D_INNER), 0.02),
        'w_att_proj': nrm(keys[14], (DEPTH, ATT_W, D_MODEL), ATT_W ** -0.5 * INIT_BETA),
        'w_ssm_proj': nrm(keys[15], (DEPTH, D_INNER, D_MODEL), D_INNER ** -0.5 * INIT_BETA),
        'w_out': nrm(keys[16], (DEPTH, D_MODEL, D_MODEL), D_MODEL ** -0.5 * INIT_BETA),
        'ln_g': 1.0 + nrm(keys[17], (DEPTH, D_MODEL), 0.02),
        'ln_b': nrm(keys[18], (DEPTH, D_MODEL), 0.02),
    }


def reference(x_prompt, x_sample, w_in, b_gate, q_norm_w, k_norm_w, conv_w, conv_b, dt_bias_fwd,
              dt_bias_bwd, a_log_fwd, a_log_bwd, d_skip, ssm_norm_w, w_att_proj, w_ssm_proj, w_out,
              ln_g, ln_b):
    xp, xs = x_prompt, x_sample
    for l in range(DEPTH):
        params = (w_in[l], b_gate[l], q_norm_w[l], k_norm_w[l], conv_w[l], conv_b[l],
                  dt_bias_fwd[l], dt_bias_bwd[l], a_log_fwd[l], a_log_bwd[l], d_skip[l],
                  ssm_norm_w[l], w_att_proj[l], w_ssm_proj[l], w_out[l], ln_g[l], ln_b[l])
        xp = _hybrid_layer(xp, *params)
        xs = _hybrid_layer(xs, *params)
    y_prompt, y_sample = xp, xs
    return (y_prompt, y_sample)
```

```python
import numpy as np
from contextlib import ExitStack
import concourse.bass as bass
import concourse.mybir as mybir
from concourse.bass_utils import run_bass_kernel_spmd

F32 = mybir.dt.float32
BF16 = mybir.dt.bfloat16
AF = mybir.ActivationFunctionType
ALU = mybir.AluOpType

D_MODEL = 1024
N_CORES = 8
GRID_W = 64
RMS_EPS = 1e-6
LN_EPS = 1e-5
ALPHA = 2.0 ** 0.25
NFM = 60
NTM = 2368
SMAX = 4096

ENGS = ['sync', 'act', 'pool', 'pe', 'dve']
PSUM_KEYS = {'pm', 'pa', 'pt', 'ptb', 'st', 'acc', 'pp', 'pss', 'pcb', 'seg', 'pyd', 'pyo', 'pst', 'pso', 'po'}
PHASES = "1234"
SEM_RESET = False
SEM_WIN = 30000
NDMASEM = 12


class Op:
    __slots__ = ('eng', 'fn', 'dma', 'deps', 'sig', 'needs_sig')

    def __init__(self, eng, fn, dma):
        self.eng = eng
        self.fn = fn
        self.dma = dma
        self.deps = []
        self.sig = None
        self.needs_sig = False


class Prog:
    def __init__(self, nc, es):
        self.nc = nc
        self.es = es
        self.ops = []
        self.last_w = {}
        self.readers = {}
        self.sems = {}
        self.cnt = {e: 0 for e in ENGS}
        self.dma_sems = {}
        self.dma_rr = {e: 0 for e in ENGS}
        self.seen = {e: {} for e in ENGS}
        self.all_dma = []
        self.barrier_ops = []
        self.nops = 0
        self.semA = self._sem("hs_a")
        self.semB = self._sem("hs_b")
        self.nphase = 0

    def _sem(self, name):
        return self.es.enter_context(self.nc.semaphore(name))

    def op(self, eng, fn, reads=(), writes=(), dma=False):
        o = Op(eng, fn, dma)
        self.nops += 1
        deps = []
        xr_ = [k for k in reads if (k if isinstance(k, str) else k[0]) in PSUM_KEYS and k not in writes]
        if xr_:
            writes = list(writes) + xr_
        for k in reads:
            w = self.last_w.get(k)
            if w is not None:
                deps.append((w, 'raw'))
        for k in writes:
            w = self.last_w.get(k)
            if w is not None:
                deps.append((w, 'waw'))
            for r in self.readers.get(k, ()):
                deps.append((r, 'war'))
        for d, kind in deps:
            if d is o:
                continue
            if d.eng == o.eng and not d.dma and not o.dma:
                if kind != 'raw' or o.eng == 'pe':
                    continue
            o.deps.append(d)
        for k in reads:
            self.readers.setdefault(k, []).append(o)
        for k in writes:
            self.last_w[k] = o
            self.readers[k] = []
        if dma:
            pool = self.dma_sems.setdefault(eng, [])
            if len(pool) < NDMASEM:
                pool.append([self._sem("d_%s_%d" % (eng, len(pool))), 0, None])
            slot = pool[self.dma_rr[eng] % NDMASEM]
            self.dma_rr[eng] += 1
            if slot[2] is not None:
                o.deps.append(slot[2])
            slot[1] += 16
            slot[2] = o
            o.sig = (slot[0], slot[1])
            o.needs_sig = True
            self.all_dma.append(o)
        self.ops.append(o)
        return o

    def emit_phase(self, final=False):
        nc = self.nc
        ops = self.ops
        for o in ops:
            for d in o.deps:
                d.needs_sig = True
        last = {}
        for o in ops:
            if not o.dma:
                last[o.eng] = o
        for o in last.values():
            o.needs_sig = True
        for o in ops:
            if o.needs_sig and not o.dma and o.sig is None:
                c = self.cnt[o.eng]
                self.cnt[o.eng] += 1
                win = c // SEM_WIN
                if (o.eng, win) not in self.sems:
                    self.sems[(o.eng, win)] = self._sem("s_%s_%d" % (o.eng, win))
                o.sig = (self.sems[(o.eng, win)], c % SEM_WIN + 1)
        streams = {e: [] for e in ENGS}
        for o in ops:
            streams[o.eng].append(o)
        bar = list(self.barrier_ops)
        nph = self.nphase
        clear_list = list(self.sems.values()) + [sl[0] for pl in self.dma_sems.values() for sl in pl]
        dma_of_phase = [o for o in ops if o.dma]
        seen = self.seen
        all_dma = self.all_dma

        def run(eng, e):
            sn = seen[eng]

            def wait(d):
                sem, val = d.sig
                if sn.get(sem, 0) >= val:
                    return
                sn[sem] = val
                e.wait_ge(sem, val)

            for d in bar:
                wait(d)
            if nph > 0 and SEM_RESET:
                e.sem_inc(self.semA, 1)
                if eng == 'sync':
                    e.wait_ge(self.semA, 5 * nph)
                    for sm in clear_list:
                        e.sem_clear(sm)
                    e.sem_inc(self.semB, 1)
                e.wait_ge(self.semB, nph)
                sn.clear()
            for o in streams[eng]:
                for d in o.deps:
                    wait(d)
                ins = o.fn(e)
                if o.needs_sig:
                    ins.then_inc(o.sig[0], 16 if o.dma else 1)
            if final and eng == 'sync':
                for d in all_dma:
                    wait(d)
                for d in last.values():
                    wait(d)

        with nc.Block() as block:
            @block.sync
            def _(e):
                run('sync', e)

            @block.scalar
            def _(e):
                run('act', e)

            @block.gpsimd
            def _(e):
                run('pool', e)

            @block.tensor
            def _(e):
                run('pe', e)

            @block.vector
            def _(e):
                run('dve', e)
        nb = {}
        for o in list(last.values()) + dma_of_phase + bar:
            s = o.sig[0]
            if s not in nb or nb[s].sig[1] < o.sig[1]:
                nb[s] = o
        self.barrier_ops = list(nb.values())
        self.ops = []
        self.last_w = {}
        self.readers = {}
        if SEM_RESET:
            self.cnt = {e: 0 for e in ENGS}
            for pl in self.dma_sems.values():
                for sl in pl:
                    sl[1] = 0
                    sl[2] = None
            self.all_dma = []
        self.nphase += 1


_UID = [0]


def _uid():
    _UID[0] += 1
    return _UID[0]


def MM(P, out, lhsT, rhs, start, stop, r, w):
    return P.op('pe', lambda e: e.matmul(out, lhsT=lhsT, rhs=rhs, start=start, stop=stop), r, w)


def TR(P, out, in_, ident, r, w):
    return P.op('pe', lambda e: e.transpose(out=out, in_=in_, identity=ident), r, w)


def ACT(P, out, in_, func, r, w, bias=None, scale=None, accum=None):
    kw = {}
    if bias is not None:
        kw['bias'] = bias
    if scale is not None:
        kw['scale'] = scale
    if accum is not None:
        kw['accum_out'] = accum
    return P.op('act', lambda e: e.activation(out=out, in_=in_, func=func, **kw), r, w)


def TT(P, eng, out, in0, in1, op, r, w):
    return P.op(eng, lambda e: e.tensor_tensor(out=out, in0=in0, in1=in1, op=op), r, w)


def TS(P, eng, out, in0, s1, op0, r, w, s2=None, op1=None):
    if op1 is None:
        return P.op(eng, lambda e: e.tensor_scalar(out=out, in0=in0, scalar1=s1, scalar2=None, op0=op0), r, w)
    return P.op(eng, lambda e: e.tensor_scalar(out=out, in0=in0, scalar1=s1, scalar2=s2, op0=op0, op1=op1), r, w)


def STT(P, eng, out, in0, scalar, in1, op0, op1, r, w):
    return P.op(eng, lambda e: e.scalar_tensor_tensor(out=out, in0=in0, scalar=scalar, in1=in1, op0=op0, op1=op1), r, w)


def CP(P, eng, out, in_, r, w):
    if eng == 'act':
        return P.op('act', lambda e: e.activation(out=out, in_=in_, func=AF.Copy), r, w)
    return P.op(eng, lambda e: e.tensor_copy(out=out, in_=in_), r, w)


def RECIP(P, out, in_, r, w):
    return P.op('dve', lambda e: e.reciprocal(out=out, in_=in_), r, w)


def MEMSET(P, eng, ap, val, w):
    return P.op(eng, lambda e: e.memset(ap, val), (), w)


def DMA(P, q, out, in_, r=(), w=()):
    return P.op(q, lambda e: e.dma_start(out=out, in_=in_), r, w, dma=True)


def build_program(seqs, dbg=False):
    ntok = sum(seqs)
    nc = bass.Bass("TRN2", target_bir_lowering=False)
    kin = "ExternalInput"

    def din(name, shape, dt=F32):
        return nc.dram_tensor(name, list(shape), dt, kind=kin).ap()

    dbg_names = []

    def dscr(name, shape, dt):
        if dbg:
            dbg_names.append(name)
            return nc.dram_tensor(name, list(shape), dt, kind="ExternalOutput").ap()
        return nc.dram_tensor(name, list(shape), dt, kind="Internal").ap()

    x_all = din("x_all", [ntok, D_MODEL])
    w_fm = din("w_fm", [NFM, 128, 1024])
    w_tm = din("w_tm", [128, 8 * NTM])
    w_att = din("w_att", [128, 8 * 1024])
    w_ssm = din("w_ssm", [128, 16 * 1024])
    w_out = din("w_out", [128, 8 * 1024])
    consts = din("consts", [128, 8 * 128])
    vecs = din("vecs", [128, 400])
    bvecs = din("bvecs", [128, 64 + 64 + 32 + 1024 + 1024])
    rope_c = din("rope_c", [128, SMAX])
    rope_s = din("rope_s", [128, SMAX])
    y_all = nc.dram_tensor("y_all", [ntok, D_MODEL], F32, kind="ExternalOutput").ap()

    wfm_b = dscr("wfm_b", [NFM, 128, 1024], BF16)
    wtm_b = dscr("wtm_b", [128, 8 * NTM], BF16)
    watt_b = dscr("watt_b", [128, 8 * 1024], BF16)
    wssm_b = dscr("wssm_b", [128, 16 * 1024], BF16)
    wout_b = dscr("wout_b", [128, 8 * 1024], BF16)
    S_ = max(seqs)
    qT_d = dscr("qT_d", [8, 128, S_], BF16)
    kTa_d = dscr("kTa_d", [2, 128, S_], BF16)
    kTb_d = dscr("kTb_d", [2, 128, S_], BF16)
    v_d = dscr("v_d", [S_, 512], BF16)
    sgT_d = dscr("sgT_d", [1024, S_], BF16)
    gT_d = dscr("gT_d", [16, 128, S_], BF16)
    sz_d = dscr("sz_d", [S_, 2048], BF16)
    xs_d = dscr("xs_d", [S_, 2048], BF16)
    Btm_d = dscr("Btm_d", [S_, 512], BF16)
    BT_d = dscr("BT_d", [4, 128, S_], BF16)
    CT_d = dscr("CT_d", [4, 128, S_], BF16)
    dt_d = dscr("dt_d", [S_, 64], F32)
    gatt_d = dscr("gatt_d", [8, 128, S_], F32)
    yb_d = dscr("yb_d", [S_, 2048], F32)
    bnc_d = dscr("bnc_d", [4, 64 * 128], BF16)
    xres = None

    with ExitStack() as es0:
        P = Prog(nc, es0)

        with ExitStack() as es:
            sb = lambda n, s, d=F32: es.enter_context(nc.sbuf_tensor(n, s, d))
            NB0 = 4096
            f_in = [sb("p0f%d" % i, [128, NB0]) for i in range(2)]
            b_out = [sb("p0b%d" % i, [128, NB0], BF16) for i in range(2)]
            vec0 = sb("p0vec", [128, 400])
            DMA(P, LDQ, vec0[:], vecs[:, :], (), ['vec0'])
            cnt = [0]
            engs = ['dve', 'act', 'dve']

            def cast_piece(src, dst, n, scale_cols=None):
                i = cnt[0] % 2
                cnt[0] += 1
                DMA(P, LDQ, f_in[i][:, 0:n], src, (), [('f', i)])
                if scale_cols is None:
                    CP(P, engs[cnt[0] % 3], b_out[i][:, 0:n], f_in[i][:, 0:n], [('f', i)], [('b', i)])
                else:
                    o = None
                    for j, sc in enumerate(scale_cols):
                        TS(P, 'dve', b_out[i][:, j * 1024:(j + 1) * 1024], f_in[i][:, j * 1024:(j + 1) * 1024],
                           vec0[:, sc:sc + 1], ALU.mult, [('f', i), 'vec0'], [('b', i, j)])
                if scale_cols is None:
                    DMA(P, STQ, dst, b_out[i][:, 0:n], [('b', i)], ())
                else:
                    DMA(P, STQ, dst, b_out[i][:, 0:n], [('b', i, j) for j in range(len(scale_cols))], [('b', i)])

            for c in range(NFM // 4):
                i = cnt[0] % 2
                cnt[0] += 1
                DMA(P, LDQ, f_in[i][:].rearrange("p (c n) -> p c n", c=4), w_fm[c * 4:(c + 1) * 4].rearrange("c p n -> p c n"),
                    (), [('f', i)])
                CP(P, engs[cnt[0] % 3], b_out[i][:], f_in[i][:], [('f', i)], [('b', i)])
                DMA(P, STQ, wfm_b[c * 4:(c + 1) * 4].rearrange("c p n -> p c n"), b_out[i][:].rearrange("p (c n) -> p c n", c=4),
                    [('b', i)], ())
            tot = 8 * NTM
            o = 0
            while o < tot:
                n = min(NB0, tot - o)
                cast_piece(w_tm[:, o:o + n], wtm_b[:, o:o + n], n)
                o += n
            for o in range(0, 8192, NB0):
                cast_piece(w_att[:, o:o + NB0], watt_b[:, o:o + NB0], NB0)
            for o in range(0, 16384, NB0):
                cast_piece(w_ssm[:, o:o + NB0], wssm_b[:, o:o + NB0], NB0,
                           scale_cols=[VEC_SNW + o // 1024 + j for j in range(4)])
            for o in range(0, 8192, NB0):
                cast_piece(w_out[:, o:o + NB0], wout_b[:, o:o + NB0], NB0)
            P.emit_phase()

        tok0 = 0
        for si, S in enumerate(seqs):
            if "1" in PHASES:
                emit_phase1(nc, P, S, tok0, locals())
            if "2" in PHASES:
                emit_phase2(nc, P, S, tok0, locals())
            if "3" in PHASES:
                emit_phase3(nc, P, S, tok0, locals(), dirn=1)
                emit_phase3(nc, P, S, tok0, locals(), dirn=0)
            if "4" in PHASES:
                emit_phase4(nc, P, S, tok0, locals(), final=False)
            tok0 += S
        P.emit_phase(final=True)
    return nc, dbg_names


VEC_QW = 0
VEC_KW = 1
VEC_CW = 2
VEC_CB = 122
VEC_BG = 146
VEC_SNW = 162
VEC_END = 178
BV_DTB = 0
BV_ALOG = 64
BV_DSK = 128
BV_LNG = 160
BV_LNB = 160 + 1024

FM_KINDS = ([('q', i) for i in range(8)] + [('ka', i) for i in range(2)] + [('kb', i) for i in range(2)]
            + [('g', i) for i in range(8)] + [('gate', i) for i in range(16)]
            + [('xs', i) for i in range(16)] + [('B', i) for i in range(4)] + [('C', i) for i in range(4)])


LDQ = 'act'
STQ = 'sync'


def emit_phase1(nc, P, S, tok0, G):
    x_all = G['x_all']
    NT = S // 128
    NB = S // 512
    with ExitStack() as es:
        _u = _uid()
        sb = lambda n, s, d=F32: es.enter_context(nc.sbuf_tensor("%s_u%d" % (n, _u), s, d))
        ps = lambda n, s, d=F32: es.enter_context(nc.psum_tensor("%s_u%d" % (n, _u), s, d))
        xT = sb("xT", [128, 8, S], BF16)
        xin = [sb("xin%d" % i, [128, 1024]) for i in range(2)]
        cst = sb("cst1", [128, 8 * 128])
        ident = cst[:, 0:128]
        cstb = sb("cst1b", [128, 8 * 128], BF16)
        identb = cstb[:, 0:128]
        permb = cstb[:, 128:256]
        bonesb = cstb[:, 256:384]
        vec = sb("vec1", [128, 400])
        bv = sb("bv1", [128, 64])
        epsb = sb("epsb", [128, 1])
        wfm = [sb("wfm%d" % i, [128, 8, 128], BF16) for i in range(4)]
        wtm = [sb("wtm%d" % i, [128, 8, 512], BF16) for i in range(2)]
        pre = [sb("pre%d" % i, [128, S + 4]) for i in range(2)]
        acc = sb("acc", [128, S])
        cvo = [sb("cvo%d" % i, [128, S], BF16) for i in range(2)]
        rc = [sb("rc%d" % i, [128, 512]) for i in range(3)]
        rs = [sb("rs%d" % i, [128, 512]) for i in range(3)]
        sqb = [sb("sqb%d" % i, [128, 512], BF16) for i in range(2)]
        rstd = [sb("rstd%d" % i, [128, 512]) for i in range(2)]
        qnb = [sb("qnb%d" % i, [128, 512], BF16) for i in range(2)]
        t1 = [sb("t1_%d" % i, [128, 512]) for i in range(2)]
        t2 = [sb("t2_%d" % i, [128, 512]) for i in range(2)]
        ost = [sb("ost%d" % i, [128, 512], BF16) for i in range(4)]
        vst = [sb("vst%d" % i, [128, 4, 128], BF16) for i in range(2)]
        dtt = [sb("dtt%d" % i, [128, 64]) for i in range(2)]
        tms = [sb("tms%d" % i, [128, 4, 128], BF16) for i in range(2)]
        pm = [ps("pm%d" % i, [128, 512]) for i in range(3)]
        pa = [ps("pa%d" % i, [128, 512]) for i in range(2)]
        pt = [ps("pt%d" % i, [128, 512]) for i in range(1)]
        ptb = [ps("ptb%d" % i, [128, 1024], BF16) for i in range(2)]

        DMA(P, LDQ, cst[:], G['consts'][:, :], (), ['cst'])
        DMA(P, LDQ, vec[:], G['vecs'][:, :], (), ['vec'])
        DMA(P, LDQ, bv[:], G['bvecs'][:, BV_DTB:BV_DTB + 64], (), ['bv'])
        CP(P, 'dve', cstb[:], cst[:], ['cst'], ['cstb'])
        MEMSET(P, 'pool', epsb[:], RMS_EPS, ['epsb'])
        for i in range(2):
            MEMSET(P, 'pool', vst[i][:], 1.0, [('vst', i)])
            MEMSET(P, 'pool', pre[i][:, 0:2], 0.0, [('pre', i, 'pad')])
            MEMSET(P, 'pool', pre[i][:, S + 2:S + 4], 0.0, [('pre', i, 'pad')])

        wtm_b = G['wtm_b'].rearrange("p (k n) -> p k n", k=8)
        groups = [('v', 0, 256)] + [('z', 256 + j * 512, 512) for j in range(4)] + [('dt', 2304, 64)]

        def load_wtm(gi):
            kind, c0, n = groups[gi]
            DMA(P, LDQ, wtm[gi % 2][:, :, 0:n], wtm_b[:, :, c0:c0 + n], (), [('wtm', gi % 2)])

        def load_wfm(c):
            DMA(P, LDQ, wfm[c % 4][:], G['wfm_b'][c].rearrange("p (k n) -> p k n", k=8), (), [('wfm', c % 4)])

        def load_x(t):
            DMA(P, LDQ, xin[t % 2][:], x_all[tok0 + t * 128: tok0 + (t + 1) * 128, :], (), [('xin', t % 2)])

        load_x(0)
        load_wtm(0)
        load_wtm(1)
        ev = 0
        trb = [(pt[0], ('pt', 0)), (pa[0], ('pa', 0)), (pa[1], ('pa', 1))]
        for t in range(NT):
            xi = t % 2
            if t + 1 < NT:
                load_x(t + 1)
            for half in range(2):
                bank, bkey = trb[ev % 3]
                for k in range(4):
                    kk = half * 4 + k
                    TR(P, bank[:, k * 128:(k + 1) * 128], xin[xi][:, kk * 128:(kk + 1) * 128], ident,
                       [('xin', xi), 'cst'], [bkey])
                eng = 'dve' if ev % 2 == 0 else 'act'
                ev += 1
                CP(P, eng, xT[:, half * 4:(half + 1) * 4, t * 128:(t + 1) * 128],
                   bank[:].rearrange("p (k n) -> p k n", k=4), [bkey], [('xT', t)])
        xT_keys = [('xT', t) for t in range(NT)]

        pmi = 0
        osti = 0
        load_wfm(0)
        load_wfm(1)
        for gi, (kind, c0, n) in enumerate(groups):
            wi = gi % 2
            if gi >= 1 and gi + 1 < len(groups):
                load_wtm(gi + 1)
            for t in range(NT):
                pb = pm[pmi % 3]
                pk = ('pm', pmi % 3)
                pmi += 1
                for k in range(8):
                    MM(P, pb[:, 0:n], xT[:, k, t * 128:(t + 1) * 128], wtm[wi][:, k, 0:n], k == 0, k == 7,
                       [('xT', t), ('wtm', wi)], [pk])
                rows = slice(t * 128, (t + 1) * 128)
                if kind == 'v':
                    vi = t % 2
                    CP(P, 'act', vst[vi][:, :, 0:64], pb[:, 0:256].rearrange("p (g d) -> p g d", g=4), [pk], [('vst', vi)])
                    DMA(P, STQ, G['v_d'][rows, :], vst[vi][:].rearrange("p g d -> p (g d)"), [('vst', vi)], ())
                elif kind == 'z':
                    oi = osti % 4
                    osti += 1
                    ACT(P, ost[oi][:], pb[:], AF.Silu, [pk], [('ost', oi)])
                    DMA(P, STQ, G['sz_d'][rows, c0 - 256:c0 - 256 + 512], ost[oi][:], [('ost', oi)], ())
                else:
                    di = t % 2
                    TT(P, 'dve', dtt[di][:], pb[:, 0:64], bv[:], ALU.add, [pk, 'bv'], [('dtt', di)])
                    ACT(P, dtt[di][:], dtt[di][:], AF.Exp, [('dtt', di)], [('dtt', di)])
                    ACT(P, dtt[di][:], dtt[di][:], AF.Ln, [('dtt', di)], [('dtt', di)], bias=1.0)
                    DMA(P, STQ, G['dt_d'][rows, :], dtt[di][:], [('dtt', di)], ())

        items = [(c, tb) for c in range(NFM) for tb in range(NB)]
        qk_items = [it for it in items if FM_KINDS[it[0]][0] in ('q', 'ka', 'kb')]
        qk_index = {it: n for n, it in enumerate(qk_items)}

        def load_rope(n):
            if n < len(qk_items):
                tb = qk_items[n][1]
                cols = slice(tb * 512, (tb + 1) * 512)
                DMA(P, LDQ, rc[n % 3][:], G['rope_c'][:, cols], (), [('rc', n % 3)])
                DMA(P, LDQ, rs[n % 3][:], G['rope_s'][:, cols], (), [('rs', n % 3)])

        load_rope(0)
        load_rope(1)
        load_rope(2)
        state = {'pmi': pmi, 'osti': osti, 'tmi': 0}
        pend_g2 = []
        pend_g3 = []
        pend_tr = []
        pend_silu = []

        def qk_g2(n, pk, pb, kind):
            j = n % 2
            wcol = VEC_QW if kind == 'q' else VEC_KW
            MM(P, pa[0][:], bonesb, sqb[j][:], True, True, [('sqb', j), 'cstb'], [('pa', 0)])
            ACT(P, rstd[j][:], pa[0][:], AF.Ln, [('pa', 0), 'epsb'], [('rstd', j)], bias=epsb[:, 0:1])
            ACT(P, rstd[j][:], rstd[j][:], AF.Exp, [('rstd', j)], [('rstd', j)], scale=-0.5)
            STT(P, 'dve', qnb[j][:], pb[:], vec[:, wcol:wcol + 1], rstd[j][:], ALU.mult, ALU.mult,
                [pk, 'vec', ('rstd', j)], [('qnb', j)])

        def qk_g3(n, kind, idx, tb):
            j = n % 2
            r3 = n % 3
            cols = slice(tb * 512, (tb + 1) * 512)
            dst = {'q': G['qT_d'], 'ka': G['kTa_d'], 'kb': G['kTb_d']}[kind]
            MM(P, pa[1][:], permb, qnb[j][:], True, True, [('qnb', j), 'cstb'], [('pa', 1)])
            TT(P, 'pool', t1[j][:], qnb[j][:], rc[r3][:], ALU.mult, [('qnb', j), ('rc', r3)], [('t1', j)])
            TT(P, 'dve', t2[j][:], pa[1][:], rs[r3][:], ALU.mult, [('pa', 1), ('rs', r3)], [('t2', j)])
            oi = state['osti'] % 4
            state['osti'] += 1
            TT(P, 'dve', ost[oi][:], t1[j][:], t2[j][:], ALU.add, [('t1', j), ('t2', j)], [('ost', oi)])
            DMA(P, STQ, dst[idx][:, cols], ost[oi][:], [('ost', oi)], ())
            load_rope(n + 3)

        def conv_tail(kind, idx, ci3):
            dst = G['xs_d'] if kind == 'xs' else G['Btm_d']
            for t4 in range(NT // 4):
                ti = state['tmi'] % 2
                state['tmi'] += 1
                for tt in range(4):
                    t = t4 * 4 + tt
                    TR(P, ptb[ti][:, tt * 128:(tt + 1) * 128], cvo[ci3][:, t * 128:(t + 1) * 128],
                       identb, [('cvo', ci3), 'cstb'], [('ptb', ti)])
                CP(P, 'act', tms[ti][:],
                   ptb[ti][:, 0:512].rearrange("p (t n) -> p t n", t=4), [('ptb', ti)], [('tms', ti)])
                DMA(P, STQ,
                    dst[t4 * 512:(t4 + 1) * 512, idx * 128:(idx + 1) * 128].rearrange("(t p) c -> p t c", p=128),
                    tms[ti][:], [('tms', ti)], ())

        nconv = 0
        hsplit = (S * 5 // 8) // 512 * 512
        for c, (kind, idx) in enumerate(FM_KINDS):
            wi = c % 4
            if c + 2 < NFM:
                load_wfm(c + 2)
            pri = c % 2
            for tb in range(NB):
                pmi_ = state['pmi']
                state['pmi'] += 1
                pb = pm[pmi_ % 3]
                pk = ('pm', pmi_ % 3)
                cols = slice(tb * 512, (tb + 1) * 512)
                for k in range(8):
                    MM(P, pb[:], wfm[wi][:, k, :], xT[:, k, cols], k == 0, k == 7,
                       [('wfm', wi)] + xT_keys[tb * 4:(tb + 1) * 4], [pk])
                if pend_g3:
                    pend_g3.pop(0)()
                if pend_g2:
                    f2, f3 = pend_g2.pop(0)
                    f2()
                    pend_g3.append(f3)
                if tb == NB - 1 and len(pend_tr) > 1:
                    pend_tr.pop(0)()
                if kind in ('q', 'ka', 'kb'):
                    n = qk_index[(c, tb)]
                    j = n % 2
                    ACT(P, sqb[j][:], pb[:], AF.Square, [pk], [('sqb', j)])
                    pend_g2.append((lambda n=n, pk=pk, pb=pb, kind=kind: qk_g2(n, pk, pb, kind),
                                    lambda n=n, kind=kind, idx=idx, tb=tb: qk_g3(n, kind, idx, tb)))
                elif kind == 'g':
                    oi = state['osti'] % 4
                    state['osti'] += 1
                    ACT(P, ost[oi][:], pb[:], AF.Silu, [pk], [('ost', oi)])
                    DMA(P, STQ, G['sgT_d'][idx * 128:(idx + 1) * 128, cols], ost[oi][:], [('ost', oi)], ())
                elif kind == 'gate':
                    oi = state['osti'] % 4
                    state['osti'] += 1
                    ACT(P, ost[oi][:], pb[:], AF.Sigmoid, [pk, 'vec'], [('ost', oi)],
                        bias=vec[:, VEC_BG + idx:VEC_BG + idx + 1])
                    DMA(P, STQ, G['gT_d'][idx][:, cols], ost[oi][:], [('ost', oi)], ())
                else:
                    CP(P, 'act', pre[pri][:, 2 + tb * 512: 2 + (tb + 1) * 512], pb[:], [pk], [('pre', pri, tb)])
            while pend_silu:
                pend_silu.pop(0)()
            if kind in ('xs', 'B', 'C'):
                cc = {'xs': 0, 'B': 16, 'C': 20}[kind] + idx
                ci3 = nconv % 2
                nconv += 1
                prk = [('pre', pri, tb) for tb in range(NB)] + [('pre', pri, 'pad')]
                w0 = VEC_CW + cc * 5
                ACT(P, acc[:], pre[pri][:, 0:S], AF.Identity, prk + ['vec'], ['acc_a'], scale=vec[:, w0:w0 + 1])
                for j in range(1, 5):
                    STT(P, 'dve', acc[:], pre[pri][:, j:S + j], vec[:, w0 + j:w0 + j + 1], acc[:],
                        ALU.mult, ALU.add, prk + ['vec', 'acc_a'], ['acc_a'])

                def silu_store(kind=kind, idx=idx, ci3=ci3, cc=cc):
                    ACT(P, cvo[ci3][:], acc[:], AF.Silu, ['acc_a', 'vec'], [('cvo', ci3)],
                        bias=vec[:, VEC_CB + cc:VEC_CB + cc + 1])
                    if kind == 'B':
                        DMA(P, STQ, G['BT_d'][idx][:, 0:S], cvo[ci3][:], [('cvo', ci3)], ())
                    if kind == 'C':
                        DMA(P, STQ, G['CT_d'][idx][:, 0:S], cvo[ci3][:], [('cvo', ci3)], ())

                pend_silu.append(silu_store)
                if kind in ('xs', 'B'):
                    pend_tr.append(lambda kind=kind, idx=idx, ci3=ci3: conv_tail(kind, idx, ci3))
                else:
                    pend_tr.append(lambda: None)
        while pend_g3 or pend_g2:
            if pend_g3:
                pend_g3.pop(0)()
            if pend_g2:
                f2, f3 = pend_g2.pop(0)
                f2()
                pend_g3.append(f3)
        while pend_silu:
            pend_silu.pop(0)()
        while pend_tr:
            pend_tr.pop(0)()
        P.emit_phase()


def emit_phase2(nc, P, S, tok0, G):
    NKB = S // 128
    NQB = S // 512
    with ExitStack() as es:
        _u = _uid()
        sb = lambda n, s, d=F32: es.enter_context(nc.sbuf_tensor("%s_u%d" % (n, _u), s, d))
        ps = lambda n, s, d=F32: es.enter_context(nc.psum_tensor("%s_u%d" % (n, _u), s, d))
        kTa = sb("kTa", [128, 2, S], BF16)
        kTb = sb("kTb", [128, 2, S], BF16)
        vv = sb("vv", [128, NKB, 512], BF16)
        watt = sb("watt", [128, 8, 1024], BF16)
        qT = [sb("qT%d" % i, [128, 8, 512], BF16) for i in range(2)]
        sg2 = [sb("sg2_%d" % i, [64, 16, 512], BF16) for i in range(2)]
        ga = [sb("ga%d" % i, [128, 8, 512], BF16) for i in range(2)]
        PT = [sb("PT%d" % i, [128, 1024], BF16) for i in range(3)]
        attT = [sb("attT%d" % i, [128, 8, 512], BF16) for i in range(2)]
        rden = [sb("rden%d" % i, [64, 512]) for i in range(2)]
        tmp = [sb("tmpa%d" % i, [64, 512]) for i in range(2)]
        gst = [sb("gst%d" % i, [128, 512]) for i in range(2)]
        st = [ps("st%d" % i, [128, 1024]) for i in range(3)]
        acc = [ps("acc%d" % i, [128, 512]) for i in range(2)]
        accs = [sb("accs%d" % i, [128, 512]) for i in range(2)]

        def load_resident():
            DMA(P, LDQ, kTa[:, :, :], G['kTa_d'][:, :, 0:S].rearrange("c p s -> p c s"), (), ['kTa'])
            DMA(P, LDQ, kTb[:, :, :], G['kTb_d'][:, :, 0:S].rearrange("c p s -> p c s"), (), ['kTb'])
            for t4 in range(0, NKB, 8):
                DMA(P, LDQ, vv[:, t4:t4 + 8, :], G['v_d'][t4 * 128:(t4 + 8) * 128, :].rearrange("(t p) c -> p t c", p=128),
                    (), [('vv', t4 // 8)])
            DMA(P, LDQ, watt[:], G['watt_b'].rearrange("p (k n) -> p k n", k=8), (), ['watt'])

        def load_q(qb):
            qi = qb % 2
            cols = slice(qb * 512, (qb + 1) * 512)
            DMA(P, LDQ, qT[qi][:], G['qT_d'][:, :, cols].rearrange("c p s -> p c s"), (), [('qT', qi)])
            DMA(P, LDQ, sg2[qi][:], G['sgT_d'][:, cols].rearrange("(h d) s -> d h s", d=64), (), [('sg2', qi)])
            DMA(P, LDQ, ga[qi][:], G['gT_d'][0:8, :, cols].rearrange("c p s -> p c s"), (), [('ga', qi)])

        units = [(qb, c, kb) for qb in range(NQB) for c in range(8) for kb in range(NKB)]

        def qk(u, ui):
            qb, c, kb = u
            qi = qb % 2
            g = c // 2
            kE, kEk = (kTa, 'kTa') if g % 2 == 0 else (kTb, 'kTb')
            kO, kOk = (kTb, 'kTb') if g % 2 == 0 else (kTa, 'kTa')
            si = ui % 3
            MM(P, st[si][:, 0:512], kE[0:64, g // 2, kb * 128:(kb + 1) * 128], qT[qi][0:64, c, :], True, True,
               [kEk, ('qT', qi)], [('st', si)])
            MM(P, st[si][:, 512:1024], kO[64:128, g // 2, kb * 128:(kb + 1) * 128], qT[qi][64:128, c, :], True, True,
               [kOk, ('qT', qi)], [('st', si)])

        def pv(u, ui):
            qb, c, kb = u
            qi = qb % 2
            g = c // 2
            si = ui % 3
            pi = ui % 3
            ACT(P, PT[pi][:], st[si][:], AF.Exp, [('st', si)], [('PT', pi)], scale=0.125)
            for par in range(2):
                ai = par
                MM(P, acc[ai][:], vv[:, kb, g * 128:(g + 1) * 128], PT[pi][:, par * 512:(par + 1) * 512],
                   kb == 0, kb == NKB - 1, [('vv', kb // 8), ('PT', pi)], [('acc', ai)])
            if kb == NKB - 1:
                for par in range(2):
                    CP(P, 'dve', accs[par][:], acc[par][:], [('acc', par)], [('accs', par)])
                for par in range(2):
                    ai = par
                    h = 2 * c + par
                    ri = par
                    RECIP(P, rden[ri][:], accs[ai][64:128, :], [('accs', ai)], [('rden', ri)])
                    TT(P, 'dve', tmp[ri][:], accs[ai][0:64, :], rden[ri][:], ALU.mult, [('accs', ai), ('rden', ri)], [('tmp', ri)])
                    TT(P, 'pool', attT[qi][par * 64:(par + 1) * 64, c, :], tmp[ri][:], sg2[qi][:, h, :], ALU.mult,
                       [('tmp', ri), ('sg2', qi)], [('attT', qi, h)])
                if c == 7:
                    cols = slice(qb * 512, (qb + 1) * 512)
                    for ec in range(8):
                        p2 = ec % 2
                        ppb = st[si][:, p2 * 512:(p2 + 1) * 512]
                        for kc in range(8):
                            MM(P, ppb, watt[:, kc, ec * 128:(ec + 1) * 128], attT[qi][:, kc, :], kc == 0, kc == 7,
                               ['watt'] + [('attT', qi, hh) for hh in range(16)], [('st', si)])
                        TT(P, 'dve', gst[p2][:], ppb, ga[qi][:, ec, :], ALU.mult, [('st', si), ('ga', qi)], [('gst', p2)])
                        DMA(P, STQ, G['gatt_d'][ec][:, cols], gst[p2][:], [('gst', p2)], ())

        load_q(0)
        load_resident()
        if NQB > 1:
            load_q(1)
        qk(units[0], 0)
        qk(units[1], 1)
        for ui, u in enumerate(units):
            if u[1] == 0 and u[2] == 0 and u[0] >= 1 and u[0] + 1 < NQB:
                load_q(u[0] + 1)
            if ui + 2 < len(units):
                qk(units[ui + 2], ui + 2)
            pv(u, ui)
        P.emit_phase()


def emit_phase3(nc, P, S, tok0, G, dirn):
    NT = S // 128
    order = list(range(NT)) if dirn == 0 else list(range(NT - 1, -1, -1))
    with ExitStack() as es:
        _u = _uid()
        sb = lambda n, s, d=F32: es.enter_context(nc.sbuf_tensor("%s_u%d" % (n, _u), s, d))
        ps = lambda n, s, d=F32: es.enter_context(nc.psum_tensor("%s_u%d" % (n, _u), s, d))
        cst = sb("cst3", [128, 8 * 128])
        cstb = sb("cst3b", [128, 8 * 128], BF16)
        tri = cst[:, (3 + dirn) * 128:(4 + dirn) * 128]
        ntri = cst[:, (6 + dirn) * 128:(7 + dirn) * 128]
        ones = cst[:, 5 * 128:6 * 128]
        maskb = cstb[:, (3 + dirn) * 128:(4 + dirn) * 128]
        bv = sb("bv3", [128, 96])
        aneg = sb("aneg", [128, 64])
        negm = sb("negm", [128, 4, 128], BF16)
        L66 = [sb("L66_%d" % i, [66, 128], BF16) for i in range(2)]
        R66 = [sb("R66_%d" % i, [66, 4096], BF16) for i in range(2)]
        xs_t = [sb("xs_t%d" % i, [128, 32, 64], BF16) for i in range(3)]
        Btm_t = [sb("Btm_t%d" % i, [128, 512], BF16) for i in range(3)]
        BT_t = [sb("BT_t%d" % i, [128, 4, 128], BF16) for i in range(3)]
        CT_t = [sb("CT_t%d" % i, [128, 4, 128], BF16) for i in range(3)]
        dt_t = [sb("dt_t%d" % i, [128, 64]) for i in range(3)]
        adt = [sb("adt%d" % i, [128, 32]) for i in range(2)]
        eacs = [sb("eacs%d" % i, [128, 32]) for i in range(2)]
        wd = [sb("wd%d" % i, [128, 32]) for i in range(2)]
        dec = [sb("dec%d" % i, [128, 32]) for i in range(2)]
        dw = [sb("dw%d" % i, [128, 32]) for i in range(2)]
        lndt = [sb("lndt%d" % i, [128, 32]) for i in range(2)]
        Lb = [sb("Lb%d" % i, [64, 128], BF16) for i in range(2)]
        xw = [sb("xw%d" % i, [128, 32, 64], BF16) for i in range(2)]
        cbm = [sb("cbm%d" % i, [128, 4, 128], BF16) for i in range(2)]
        Lm = [sb("Lm%d" % i, [128, 4, 128], BF16) for i in range(2)]
        sc = [sb("sc%d" % i, [128, 4, 128], BF16) for i in range(2)]
        yo = [sb("yo%d" % i, [128, 8, 64]) for i in range(2)]
        ydir = [sb("ydir%d" % i, [128, 2048]) for i in range(2)]
        hf = sb("hf", [128, 32, 64])
        hb = sb("hb", [128, 2048], BF16)
        if dirn == 0:
            yb_t = [sb("yb_t%d" % i, [128, 2048]) for i in range(3)]
            Dd = sb("Dd", [128, 32, 128], BF16)
        pss = ps("pss", [128, 512])
        pcb = ps("pcb", [128, 512])
        seg = [ps("seg%d" % i, [128, 512]) for i in range(2)]
        pyd2 = [ps("pyd%d" % i, [128, 512]) for i in range(2)]
        pyo = ps("pyo", [128, 512])
        pst = ps("pst", [128, 512])

        DMA(P, LDQ, cst[:], G['consts'][:, :], (), ['cst'])
        DMA(P, LDQ, bv[:], G['bvecs'][:, BV_ALOG:BV_ALOG + 96], (), ['bv'])
        CP(P, 'dve', cstb[:], cst[:], ['cst'], ['cstb'])
        TS(P, 'dve', negm[:], tri.unsqueeze(1).to_broadcast([128, 4, 128]), -1.0, ALU.add, ['cst'], ['negm'],
           s2=30000.0, op1=ALU.mult)
        ACT(P, aneg[:], bv[:, 0:64], AF.Exp, ['bv'], ['aneg'])
        TS(P, 'dve', aneg[:], aneg[:], -1.0, ALU.mult, ['aneg'], ['aneg'])
        for i in range(2):
            MEMSET(P, 'pool', L66[i][64:66, :], -1.0, [('L66c', i)])
            for hh in range(2):
                CP(P, 'dve', R66[i][hh * 32:(hh + 1) * 32, :].rearrange("p (h i) -> p h i", h=32),
                   cstb[hh * 32:(hh + 1) * 32, hh * 32:(hh + 1) * 32].unsqueeze(2).to_broadcast([32, 32, 128]),
                   ['cstb'], [('R66c', i, hh)])
        if dirn == 0:
            TT(P, 'dve', Dd[:], cstb[:, 0:128].unsqueeze(1).to_broadcast([128, 32, 128]),
               bv[:, 64:96].unsqueeze(2).to_broadcast([128, 32, 128]), ALU.mult, ['cstb', 'bv'], ['Dd'])
        MEMSET(P, 'pool', hf[:], 0.0, [('hf', g) for g in range(4)])
        MEMSET(P, 'pool', hb[:], 0.0, [('hb', g) for g in range(4)])
        dcol = dirn * 32

        def loads(ci):
            t = order[ci]
            l = ci % 3
            rows = slice(t * 128, (t + 1) * 128)
            cols = slice(t * 128, (t + 1) * 128)
            DMA(P, LDQ, xs_t[l][:].rearrange("p h d -> p (h d)"), G['xs_d'][rows, :], (), [('xs_t', l)])
            DMA(P, LDQ, Btm_t[l][:], G['Btm_d'][rows, :], (), [('Btm_t', l)])
            DMA(P, LDQ, BT_t[l][:], G['BT_d'][:, :, cols].rearrange("g p s -> p g s"), (), [('BT_t', l)])
            DMA(P, LDQ, CT_t[l][:], G['CT_d'][:, :, cols].rearrange("g p s -> p g s"), (), [('CT_t', l)])
            DMA(P, LDQ, dt_t[l][:], G['dt_d'][rows, :], (), [('dt_t', l)])
            if dirn == 0:
                DMA(P, LDQ, yb_t[l][:], G['yb_d'][rows, :], [('ybd', t)], [('yb_t', l)])

        def prologue(ci):
            t = order[ci]
            i = ci % 2
            l = ci % 3
            TT(P, 'dve', adt[i][:], dt_t[l][:, dcol:dcol + 32], aneg[:, dcol:dcol + 32], ALU.mult,
               [('dt_t', l), 'aneg'], [('adt', i)])
            ACT(P, lndt[i][:], dt_t[l][:, dcol:dcol + 32], AF.Ln, [('dt_t', l)], [('lndt', i)])
            ACT(P, lndt[i][:], lndt[i][:], AF.Identity, [('lndt', i)], [('lndt', i)], scale=-1.0)
            MM(P, pss[0:32, 0:128], adt[i][:], tri, True, True, [('adt', i), 'cst'], ['pss'])
            MM(P, pss[0:32, 224:352], adt[i][:], tri, True, False, [('adt', i), 'cst'], ['pss'])
            MM(P, pss[0:32, 224:352], lndt[i][:], cst[:, 0:128], False, True, [('lndt', i), 'cst'], ['pss'])
            MM(P, pss[:, 128:160], tri, adt[i][:], True, True, [('adt', i), 'cst'], ['pss'])
            MM(P, pss[:, 160:192], ones, adt[i][:], True, True, [('adt', i), 'cst'], ['pss'])
            MM(P, pss[:, 192:224], ntri, adt[i][:], True, True, [('adt', i), 'cst'], ['pss'])
            ACT(P, Lb[i][0:32, :], pss[0:32, 0:128], AF.Identity, ['pss'], [('Lba', i)], scale=-1.0)
            STT(P, 'dve', Lb[i][32:64, :], pss[0:32, 0:128], -1.0, Lb[i][0:32, :], ALU.mult, ALU.subtract,
                ['pss', ('Lba', i)], [('Lbb', i)])
            ACT(P, L66[i][0:32, :], pss[0:32, 224:352], AF.Identity, ['pss'], [('L66a', i)], scale=-1.0)
            STT(P, 'dve', L66[i][32:64, :], pss[0:32, 224:352], -1.0, L66[i][0:32, :], ALU.mult, ALU.subtract,
                ['pss', ('L66a', i)], [('L66b', i)])
            ACT(P, eacs[i][:], pss[:, 128:160], AF.Exp, ['pss'], [('eacs', i)])
            ACT(P, dec[i][:], pss[:, 160:192], AF.Exp, ['pss'], [('dec', i)])
            ACT(P, wd[i][:], pss[:, 192:224], AF.Exp, ['pss'], [('wd', i)])
            TT(P, 'dve', dw[i][:], dt_t[l][:, dcol:dcol + 32], wd[i][:], ALU.mult, [('dt_t', l), ('wd', i)], [('dw', i)])
            DMA(P, STQ, G['bnc_d'][i].rearrange("(p n) -> p n", p=64), Lb[i][0:64, :],
                [('Lba', i), ('Lbb', i)], [('bnc', i)])
            DMA(P, STQ, R66[i][64:66, :], G['bnc_d'][i].rearrange("(p n) -> p n", p=2), [('bnc', i)], [('R66', i)])
            TT(P, 'pool', xw[i][:], xs_t[l][:], dw[i][:].unsqueeze(2).to_broadcast([128, 32, 64]), ALU.mult,
               [('xs_t', l), ('dw', i)], [('xw', i)])
            for g in range(4):
                MM(P, pcb[:, g * 128:(g + 1) * 128], BT_t[l][:, g, :], CT_t[l][:, g, :], True, True,
                   [('BT_t', l), ('CT_t', l)], ['pcb'])
            CP(P, 'act', cbm[i][:], pcb[:].rearrange("p (g n) -> p g n", g=4), ['pcb'], [('cbm', i)])

        def seg_mm(u):
            ci, q8 = u // 8, u % 8
            i = ci % 2
            si = u % 2
            MM(P, seg[si][:], L66[i][0:66, :], R66[i][0:66, q8 * 512:(q8 + 1) * 512], True, False,
               [('L66a', i), ('L66b', i), ('L66c', i), ('R66', i), ('R66c', i, 0), ('R66c', i, 1)], [('seg', si)])
            MM(P, seg[si][:], cstb[:, 0:128], negm[:].rearrange("p h n -> p (h n)"), False, True,
               ['cstb', 'negm'], [('seg', si)])

        def unit(u):
            ci, q8 = u // 8, u % 8
            t = order[ci]
            i = ci % 2
            l = ci % 3
            g = q8 // 2
            si = u % 2
            li = u % 2
            pyd = pyd2[g % 2]
            pydk = ('pyd', g % 2)
            ACT(P, Lm[li][:].rearrange("p h n -> p (h n)"), seg[si][:], AF.Exp, [('seg', si)], [('Lm', li)])
            if q8 % 2 == 1:
                while pend_hb:
                    pend_hb.pop(0)()
            TT(P, 'dve', sc[li][:], Lm[li][:], cbm[i][:, g:g + 1, :].to_broadcast([128, 4, 128]), ALU.mult,
               [('Lm', li), ('cbm', i)], [('sc', li)])
            for hh in range(4):
                h = q8 * 4 + hh
                c0 = (q8 % 2) * 256 + hh * 64
                MM(P, pyd[:, c0:c0 + 64], sc[li][:, hh, :], xs_t[l][:, h, :], True, dirn == 1,
                   [('sc', li), ('xs_t', l)], [pydk])
                if dirn == 0:
                    MM(P, pyd[:, c0:c0 + 64], Dd[:, h, :], xs_t[l][:, h, :], False, True,
                       ['Dd', ('xs_t', l)], [pydk])
            while pend_e:
                pend_e.pop(0)()
            if q8 % 2 == 1:
                yi = g % 2
                MM(P, pyo[:], CT_t[l][:, g, :], hb[:, g * 512:(g + 1) * 512], True, True,
                   [('CT_t', l), ('hb', g)], ['pyo'])
                MM(P, pst[:], Btm_t[l][:, g * 128:(g + 1) * 128], xw[i][:, g * 8:(g + 1) * 8, :].rearrange("p h d -> p (h d)"),
                   True, True, [('Btm_t', l), ('xw', i)], ['pst'])

                def group_end(g=g, i=i, l=l, t=t, yi=yi, pyd=pyd, pydk=pydk):
                    TT(P, 'pool', hf[:, g * 8:(g + 1) * 8, :], hf[:, g * 8:(g + 1) * 8, :],
                       dec[i][:, g * 8:(g + 1) * 8].unsqueeze(2).to_broadcast([128, 8, 64]), ALU.mult,
                       [('hf', g), ('dec', i)], [('hf', g)])
                    TT(P, 'dve', yo[yi][:], pyo[:].rearrange("p (h d) -> p h d", h=8),
                       eacs[i][:, g * 8:(g + 1) * 8].unsqueeze(2).to_broadcast([128, 8, 64]), ALU.mult,
                       ['pyo', ('eacs', i)], [('yo', yi)])
                    TT(P, 'dve', ydir[i][:, g * 512:(g + 1) * 512], pyd[:], yo[yi][:].rearrange("p h d -> p (h d)"), ALU.add,
                       [pydk, ('yo', yi)], [('ydir', i, g)])
                    TT(P, 'dve', hf[:, g * 8:(g + 1) * 8, :], hf[:, g * 8:(g + 1) * 8, :],
                       pst[:].rearrange("p (h d) -> p h d", h=8), ALU.add, [('hf', g), 'pst'], [('hf', g)])
                    pend_hb.append(lambda: CP(P, 'pool', hb[:, g * 512:(g + 1) * 512],
                                              hf[:, g * 8:(g + 1) * 8, :].rearrange("p h d -> p (h d)"),
                                              [('hf', g)], [('hb', g)]))
                    if g == 3:
                        rows = slice(t * 128, (t + 1) * 128)
                        ykeys = [('ydir', i, gg) for gg in range(4)]
                        if dirn == 1:
                            DMA(P, STQ, G['yb_d'][rows, :], ydir[i][:], ykeys, ())
                        else:
                            TT(P, 'pool', ydir[i][:], ydir[i][:], yb_t[l][:], ALU.add, ykeys + [('yb_t', l)],
                               [('ysum', i)] + ykeys)
                            DMA(P, STQ, G['yb_d'][rows, :], ydir[i][:], [('ysum', i)] + ykeys, [('ybd', t)])

                pend_e.append(group_end)

        NU = NT * 8
        pend_hb = []
        pend_e = []
        loads(0)
        if NT > 1:
            loads(1)
        prologue(0)
        seg_mm(0)
        for u in range(NU):
            ci, q8 = u // 8, u % 8
            if q8 == 0:
                while pend_e:
                    pend_e.pop(0)()
                if ci + 2 < NT:
                    loads(ci + 2)
                if ci + 1 < NT:
                    prologue(ci + 1)
            if u + 1 < NU:
                seg_mm(u + 1)
            unit(u)
        while pend_e:
            pend_e.pop(0)()
        while pend_hb:
            pend_hb.pop(0)()
        P.emit_phase()


def emit_phase4(nc, P, S, tok0, G, final):
    NB = S // 512
    with ExitStack() as es:
        _u = _uid()
        sb = lambda n, s, d=F32: es.enter_context(nc.sbuf_tensor("%s_u%d" % (n, _u), s, d))
        ps = lambda n, s, d=F32: es.enter_context(nc.psum_tensor("%s_u%d" % (n, _u), s, d))
        cst = sb("cst4", [128, 128])
        identb = sb("cst4b", [128, 128], BF16)
        bv = sb("bv4", [128, 2048])
        wssm = sb("wssm", [128, 16, 1024], BF16)
        wout = sb("wout", [128, 8, 1024], BF16)
        y_t = [sb("y_t%d" % i, [128, 4, 512]) for i in range(2)]
        sz_t = [sb("sz_t%d" % i, [128, 4, 512], BF16) for i in range(2)]
        yz = [sb("yz%d" % i, [128, 4, 512]) for i in range(2)]
        junk = sb("junk4", [128, 1024], BF16)
        ss = [sb("ss%d" % i, [128, 4]) for i in range(2)]
        rs = [sb("rs4_%d" % i, [128, 4]) for i in range(2)]
        yn = [sb("yn%d" % i, [128, 4, 512], BF16) for i in range(4)]
        ynT = [sb("ynT%d" % i, [128, 16, 512], BF16) for i in range(2)]
        gs = [sb("gs%d" % i, [128, 8, 512], BF16) for i in range(1)]
        gat = [sb("gat%d" % i, [128, 8, 512]) for i in range(1)]
        m1 = [sb("m1_%d" % i, [128, 512]) for i in range(2)]
        mixT = [sb("mixT%d" % i, [128, 8, 512], BF16) for i in range(1)]
        xr = [sb("xr%d" % i, [128, 1024]) for i in range(2)]
        rr = [sb("rr%d" % i, [128, 1024]) for i in range(2)]
        st1 = [sb("st1_%d" % i, [128, 8]) for i in range(2)]
        lno = [sb("lno%d" % i, [128, 1024]) for i in range(2)]
        ptb = [ps("ptb4_%d" % i, [128, 1024], BF16) for i in range(2)]
        pso = [ps("pso%d" % i, [128, 512]) for i in range(2)]
        po = [ps("po%d" % i, [128, 512]) for i in range(2)]

        DMA(P, LDQ, cst[:], G['consts'][:, 0:128], (), ['cst'])
        DMA(P, LDQ, bv[:], G['bvecs'][:, BV_LNG:BV_LNG + 2048], (), ['bv'])
        CP(P, 'dve', identb[:], cst[:], ['cst'], ['identb'])
        cnt = [0]

        def front_elem(tb, tt):
            t = tb * 4 + tt
            i = t % 2
            rows = slice(t * 128, (t + 1) * 128)
            DMA(P, LDQ, y_t[i][:].rearrange("p g n -> p (g n)"), G['yb_d'][rows, :], (), [('y_t', i)])
            DMA(P, LDQ, sz_t[i][:].rearrange("p g n -> p (g n)"), G['sz_d'][rows, :], (), [('sz_t', i)])
            TT(P, 'dve', yz[i][:], y_t[i][:], sz_t[i][:], ALU.mult, [('y_t', i), ('sz_t', i)], [('yz', i)])
            for g in range(4):
                ACT(P, junk[:, 0:512], yz[i][:, g, :], AF.Square, [('yz', i)], ['junk', ('ss', i, g)],
                    accum=ss[i][:, g:g + 1])
            ACT(P, rs[i][:], ss[i][:], AF.Sqrt, [('ss', i, g) for g in range(4)], [('rs', i)],
                scale=1.0 / 512, bias=RMS_EPS)
            RECIP(P, rs[i][:], rs[i][:], [('rs', i)], [('rs', i)])
            TT(P, 'pool', yn[tt][:], yz[i][:], rs[i][:].unsqueeze(2).to_broadcast([128, 4, 512]), ALU.mult,
               [('yz', i), ('rs', i)], [('yn', tt)])

        def front_tr(tb, tt):
            bi = tb % 2
            t = tb * 4 + tt
            i = tt
            for k4 in range(4):
                pi = cnt[0] % 2
                cnt[0] += 1
                for k in range(4):
                    kc = k4 * 4 + k
                    TR(P, ptb[pi][:, k * 128:(k + 1) * 128], yn[i][:, kc // 4, (kc % 4) * 128:(kc % 4 + 1) * 128],
                       identb[:], [('yn', i), 'identb'], [('ptb', pi)])
                CP(P, 'act' if k4 % 2 else 'dve', ynT[bi][:, k4 * 4:(k4 + 1) * 4, tt * 128:(tt + 1) * 128],
                   ptb[pi][:, 0:512].rearrange("p (k n) -> p k n", k=4), [('ptb', pi)], [('ynT', bi, tt, k4)])

        def load_gates(tb):
            cols = slice(tb * 512, (tb + 1) * 512)
            DMA(P, LDQ, gs[0][:], G['gT_d'][8:16, :, cols].rearrange("c p s -> p c s"), (), [('gs', 0)])
            DMA(P, LDQ, gat[0][:], G['gatt_d'][:, :, cols].rearrange("c p s -> p c s"), (), [('gat', 0)])

        def back(tb):
            bi = tb % 2
            cols = slice(tb * 512, (tb + 1) * 512)
            ynk = [('ynT', bi, tt, k4) for tt in range(4) for k4 in range(4)]
            for ec in range(8):
                p2 = ec % 2
                for kc in range(16):
                    MM(P, pso[p2][:], wssm[:, kc, ec * 128:(ec + 1) * 128], ynT[bi][:, kc, :], kc == 0, kc == 15,
                       ['wssm'] + ynk, [('pso', p2)])
                TT(P, 'dve', m1[p2][:], pso[p2][:], gs[0][:, ec, :], ALU.mult, [('pso', p2), ('gs', 0)], [('m1', p2)])
                TT(P, 'pool', mixT[0][:, ec, :], m1[p2][:], gat[0][:, ec, :], ALU.add, [('m1', p2), ('gat', 0)],
                   [('mixT', 0, ec)])
                if ec % 2 == 1 and tb + 1 < NB:
                    front_elem(tb + 1, ec // 2)
            mk = [('mixT', 0, ec) for ec in range(8)]
            if tb + 1 < NB:
                load_gates(tb + 1)
                for tt in range(4):
                    front_tr(tb + 1, tt)
            for tt in range(4):
                t = tb * 4 + tt
                i = t % 2
                grow = slice(tok0 + t * 128, tok0 + (t + 1) * 128)
                DMA(P, LDQ, xr[i][:], G['x_all'][grow, :], (), [('xr', i)])
                for half in range(2):
                    for ke in range(8):
                        MM(P, po[half][:], mixT[0][:, ke, tt * 128:(tt + 1) * 128], wout[:, ke, half * 512:(half + 1) * 512],
                           ke == 0, ke == 7, ['wout'] + mk, [('po', half)])
                    STT(P, 'dve', rr[i][:, half * 512:(half + 1) * 512], xr[i][:, half * 512:(half + 1) * 512], ALPHA,
                        po[half][:], ALU.mult, ALU.add, [('xr', i), ('po', half)], [('rr', i, half)])
                rk = [('rr', i, 0), ('rr', i, 1)]
                ACT(P, junk[:], rr[i][:], AF.Identity, rk, ['junk', ('st1', i, 0)], accum=st1[i][:, 0:1])
                ACT(P, junk[:], rr[i][:], AF.Square, rk, ['junk', ('st1', i, 1)], accum=st1[i][:, 1:2])
                sk_ = [('st1', i, 0), ('st1', i, 1)]
                TS(P, 'dve', st1[i][:, 2:4], st1[i][:, 0:2], 1.0 / 1024, ALU.mult, sk_, [('st1', i, 2)])
                TT(P, 'dve', st1[i][:, 4:5], st1[i][:, 2:3], st1[i][:, 2:3], ALU.mult, [('st1', i, 2)], [('st1', i, 4)])
                TT(P, 'dve', st1[i][:, 5:6], st1[i][:, 3:4], st1[i][:, 4:5], ALU.subtract, [('st1', i, 2), ('st1', i, 4)],
                   [('st1', i, 5)])
                ACT(P, st1[i][:, 6:7], st1[i][:, 5:6], AF.Sqrt, [('st1', i, 5)], [('st1', i, 6)], bias=LN_EPS)
                RECIP(P, st1[i][:, 6:7], st1[i][:, 6:7], [('st1', i, 6)], [('st1', i, 6)])
                STT(P, 'dve', st1[i][:, 7:8], st1[i][:, 2:3], -1.0, st1[i][:, 6:7], ALU.mult, ALU.mult,
                    [('st1', i, 2), ('st1', i, 6)], [('st1', i, 7)])
                ACT(P, lno[i][:], rr[i][:], AF.Identity, rk + [('st1', i, 6), ('st1', i, 7)], [('lno', i)],
                    scale=st1[i][:, 6:7], bias=st1[i][:, 7:8])
                TT(P, 'dve', lno[i][:], lno[i][:], bv[:, 0:1024], ALU.mult, [('lno', i), 'bv'], [('lno', i)])
                TT(P, 'pool', lno[i][:], lno[i][:], bv[:, 1024:2048], ALU.add, [('lno', i), 'bv'], [('lno', i)])
                DMA(P, STQ, G['y_all'][grow, :], lno[i][:], [('lno', i)], ())

        for tt in range(4):
            front_elem(0, tt)
        DMA(P, LDQ, wssm[:], G['wssm_b'].rearrange("p (k n) -> p k n", k=16), (), ['wssm'])
        load_gates(0)
        DMA(P, LDQ, wout[:], G['wout_b'].rearrange("p (k n) -> p k n", k=8), (), ['wout'])
        for tt in range(4):
            front_tr(0, tt)
        for tb in range(NB):
            back(tb)
        P.emit_phase(final=final)


def _perm64():
    return np.array([2 * i for i in range(32)] + [2 * i + 1 for i in range(32)])


def _host_layouts(w_in, b_gate, q_norm_w, k_norm_w, conv_w, conv_b, dt_bias_fwd, dt_bias_bwd, a_log_fwd,
                  a_log_bwd, d_skip, ssm_norm_w, w_att_proj, w_ssm_proj, w_out, ln_g, ln_b):
    f = np.float32
    W = np.asarray(w_in[0], f)
    pm = _perm64()
    q_cols = np.concatenate([h * 64 + pm for h in range(16)])
    ka_cols = np.concatenate([1024 + g * 64 + pm for g in (0, 1, 2, 3)])
    kb_cols = np.concatenate([1024 + g * 64 + pm for g in (1, 0, 3, 2)])
    g_cols = 1536 + np.arange(1024)
    gate_cols = 7744 + np.arange(2048)
    xbc_cols = 4608 + np.arange(3072)
    fm_cols = np.concatenate([q_cols, ka_cols, kb_cols, g_cols, gate_cols, xbc_cols])
    assert fm_cols.shape[0] == NFM * 128
    wf = W[:, fm_cols].reshape(8, 128, NFM, 128).transpose(2, 1, 0, 3)
    w_fm = np.ascontiguousarray(wf).reshape(NFM, 128, 1024)
    tm_cols = np.concatenate([1280 + np.arange(256), 2560 + np.arange(2048), 7680 + np.arange(64)])
    wt = W[:, tm_cols].reshape(8, 128, NTM).transpose(1, 0, 2)
    w_tm = np.ascontiguousarray(wt).reshape(128, 8 * NTM)

    def kmajor(w, nk):
        n = w.shape[1]
        return np.ascontiguousarray(np.asarray(w, f).reshape(nk, 128, n).transpose(1, 0, 2)).reshape(128, nk * n)

    vecs = np.zeros((128, 400), f)
    vecs[:, VEC_QW] = np.asarray(q_norm_w[0], f)[np.tile(pm, 2)]
    vecs[:, VEC_KW] = np.asarray(k_norm_w[0], f)[np.tile(pm, 2)]
    cw = np.asarray(conv_w[0], f)
    vecs[:, VEC_CW:VEC_CW + 120] = cw.reshape(5, 24, 128).transpose(2, 1, 0).reshape(128, 120)
    vecs[:, VEC_CB:VEC_CB + 24] = np.asarray(conv_b[0], f).reshape(24, 128).T
    vecs[:, VEC_BG:VEC_BG + 16] = np.asarray(b_gate[0], f).reshape(16, 128).T
    vecs[:, VEC_SNW:VEC_SNW + 16] = np.asarray(ssm_norm_w[0], f).reshape(16, 128).T
    row = np.concatenate([np.asarray(dt_bias_fwd[0], f), np.asarray(dt_bias_bwd[0], f),
                          np.asarray(a_log_fwd[0], f), np.asarray(a_log_bwd[0], f),
                          np.asarray(d_skip[0], f), np.asarray(ln_g[0], f), np.asarray(ln_b[0], f)])
    bvecs = np.ascontiguousarray(np.tile(row[None, :], (128, 1)))
    ident = np.eye(128, dtype=f)
    perm = np.zeros((128, 128), f)
    for d in range(128):
        blk, dd = d // 64, d % 64
        perm[blk * 64 + (dd + 32) % 64, d] = 1.0
    bones = np.zeros((128, 128), f)
    bones[:64, :64] = 1.0 / 64
    bones[64:, 64:] = 1.0 / 64
    tt, ii = np.meshgrid(np.arange(128), np.arange(128), indexing='ij')
    tri_f = (tt <= ii).astype(f)
    tri_b = (tt >= ii).astype(f)
    ones = np.ones((128, 128), f)
    consts = np.ascontiguousarray(np.concatenate([ident, perm, bones, tri_f, tri_b, ones, 1 - tri_f, 1 - tri_b], axis=1))
    s = np.arange(SMAX)
    rowp = (s // GRID_W).astype(f)
    colp = (s % GRID_W).astype(f)
    freqs = (10000.0 ** (-np.arange(0, 32, 2, dtype=f) / f(32))).astype(f)
    ang = np.concatenate([rowp[:, None] * freqs, colp[:, None] * freqs], axis=-1).astype(f)
    cos = np.cos(ang).astype(f).T
    sin = np.sin(ang).astype(f).T
    rope_c = np.zeros((128, SMAX), f)
    rope_s = np.zeros((128, SMAX), f)
    for p in range(128):
        dd = p % 64
        i = dd % 32
        rope_c[p] = cos[i]
        rope_s[p] = -sin[i] if dd < 32 else sin[i]
    return dict(w_fm=w_fm, w_tm=w_tm, w_att=kmajor(w_att_proj[0], 8), w_ssm=kmajor(w_ssm_proj[0], 16),
                w_out=kmajor(w_out[0], 8), consts=consts, vecs=vecs, bvecs=bvecs, rope_c=rope_c, rope_s=rope_s)


_CACHE = {}


def kernel(x_prompt, x_sample, **params):
    xp = np.asarray(x_prompt, np.float32)
    xs = np.asarray(x_sample, np.float32)
    shared = _host_layouts(**params)
    bp, sp = xp.shape[0], xp.shape[1]
    bs, ss = xs.shape[0], xs.shape[1]
    npc, nsc = bp // N_CORES, bs // N_CORES
    seqs = [sp] * npc + [ss] * nsc
    if 'nc' not in _CACHE:
        _CACHE['nc'] = build_program(seqs)[0]
    nc = _CACHE['nc']
    in_maps = []
    for c in range(N_CORES):
        parts = [xp[c * npc + i] for i in range(npc)] + [xs[c * nsc + i] for i in range(nsc)]
        m = dict(shared)
        m['x_all'] = np.ascontiguousarray(np.concatenate(parts, axis=0))
        in_maps.append(m)
    res = run_bass_kernel_spmd(nc, in_maps, core_ids=list(range(N_CORES)))
    yp = np.empty_like(xp)
    ys = np.empty_like(xs)
    for c in range(N_CORES):
        y = res.results[c]["y_all"]
        o = 0
        for i in range(npc):
            yp[c * npc + i] = y[o:o + sp]
            o += sp
        for i in range(nsc):
            ys[c * nsc + i] = y[o:o + ss]
            o += ss
    return (yp, ys)
```

```python
import numpy as np
from contextlib import ExitStack
import concourse.bass as bass
import concourse.mybir as mybir
from concourse.bass_utils import run_bass_kernel_spmd

F32 = mybir.dt.float32
BF16 = mybir.dt.bfloat16
AF = mybir.ActivationFunctionType
ALU = mybir.AluOpType

D_MODEL = 1024
N_CORES = 8
GRID_W = 64
RMS_EPS = 1e-6
LN_EPS = 1e-5
ALPHA = 2.0 ** 0.25
NFM = 60
NTM = 2368
SMAX = 4096

ENGS = ['sync', 'act', 'pool', 'pe', 'dve']
PSUM_KEYS = {'pm', 'pa', 'pt', 'ptb', 'st', 'acc', 'pp', 'pss', 'pcb', 'seg', 'pyd', 'pyo', 'pst', 'pso', 'po'}
PHASES = "1234"
SEM_RESET = False
SEM_WIN = 30000
NDMASEM = 24


class Op:
    __slots__ = ('eng', 'fn', 'dma', 'deps', 'sig', 'needs_sig')

    def __init__(self, eng, fn, dma):
        self.eng = eng
        self.fn = fn
        self.dma = dma
        self.deps = []
        self.sig = None
        self.needs_sig = False


class Prog:
    def __init__(self, nc, es):
        self.nc = nc
        self.es = es
        self.ops = []
        self.last_w = {}
        self.readers = {}
        self.sems = {}
        self.cnt = {e: 0 for e in ENGS}
        self.dma_sems = {}
        self.dma_rr = {e: 0 for e in ENGS}
        self.seen = {e: {} for e in ENGS}
        self.all_dma = []
        self.barrier_ops = []
        self.nops = 0
        self.semA = self._sem("hs_a")
        self.semB = self._sem("hs_b")
        self.nphase = 0

    def _sem(self, name):
        return self.es.enter_context(self.nc.semaphore(name))

    def op(self, eng, fn, reads=(), writes=(), dma=False):
        o = Op(eng, fn, dma)
        self.nops += 1
        deps = []
        xr_ = [k for k in reads if (k if isinstance(k, str) else k[0]) in PSUM_KEYS and k not in writes]
        if xr_:
            writes = list(writes) + xr_
        for k in reads:
            w = self.last_w.get(k)
            if w is not None:
                deps.append((w, 'raw'))
        for k in writes:
            w = self.last_w.get(k)
            if w is not None:
                deps.append((w, 'waw'))
            for r in self.readers.get(k, ()):
                deps.append((r, 'war'))
        for d, kind in deps:
            if d is o:
                continue
            if d.eng == o.eng and not d.dma and not o.dma:
                if kind != 'raw' or o.eng == 'pe':
                    continue
            o.deps.append(d)
        for k in reads:
            self.readers.setdefault(k, []).append(o)
        for k in writes:
            self.last_w[k] = o
            self.readers[k] = []
        if dma:
            pool = self.dma_sems.setdefault(eng, [])
            if len(pool) < NDMASEM:
                pool.append([self._sem("d_%s_%d" % (eng, len(pool))), 0, None])
            slot = pool[self.dma_rr[eng] % NDMASEM]
            self.dma_rr[eng] += 1
            if slot[2] is not None:
                o.deps.append(slot[2])
            slot[1] += 16
            slot[2] = o
            o.sig = (slot[0], slot[1])
            o.needs_sig = True
            self.all_dma.append(o)
        self.ops.append(o)
        return o

    def emit_phase(self, final=False):
        nc = self.nc
        ops = self.ops
        for o in ops:
            for d in o.deps:
                d.needs_sig = True
        last = {}
        for o in ops:
            if not o.dma:
                last[o.eng] = o
        for o in last.values():
            o.needs_sig = True
        for o in ops:
            if o.needs_sig and not o.dma and o.sig is None:
                c = self.cnt[o.eng]
                self.cnt[o.eng] += 1
                win = c // SEM_WIN
                if (o.eng, win) not in self.sems:
                    self.sems[(o.eng, win)] = self._sem("s_%s_%d" % (o.eng, win))
                o.sig = (self.sems[(o.eng, win)], c % SEM_WIN + 1)
        streams = {e: [] for e in ENGS}
        for o in ops:
            streams[o.eng].append(o)
        bar = list(self.barrier_ops)
        nph = self.nphase
        clear_list = list(self.sems.values()) + [sl[0] for pl in self.dma_sems.values() for sl in pl]
        dma_of_phase = [o for o in ops if o.dma]
        seen = self.seen
        all_dma = self.all_dma

        def run(eng, e):
            sn = seen[eng]

            def wait(d):
                sem, val = d.sig
                if sn.get(sem, 0) >= val:
                    return
                sn[sem] = val
                e.wait_ge(sem, val)

            for d in bar:
                wait(d)
            if nph > 0 and SEM_RESET:
                e.sem_inc(self.semA, 1)
                if eng == 'sync':
                    e.wait_ge(self.semA, 5 * nph)
                    for sm in clear_list:
                        e.sem_clear(sm)
                    e.sem_inc(self.semB, 1)
                e.wait_ge(self.semB, nph)
                sn.clear()
            for o in streams[eng]:
                for d in o.deps:
                    wait(d)
                ins = o.fn(e)
                if o.needs_sig:
                    ins.then_inc(o.sig[0], 16 if o.dma else 1)
            if final and eng == 'sync':
                for d in all_dma:
                    wait(d)
                for d in last.values():
                    wait(d)

        with nc.Block() as block:
            @block.sync
            def _(e):
                run('sync', e)

            @block.scalar
            def _(e):
                run('act', e)

            @block.gpsimd
            def _(e):
                run('pool', e)

            @block.tensor
            def _(e):
                run('pe', e)

            @block.vector
            def _(e):
                run('dve', e)
        nb = {}
        for o in list(last.values()) + dma_of_phase + bar:
            s = o.sig[0]
            if s not in nb or nb[s].sig[1] < o.sig[1]:
                nb[s] = o
        self.barrier_ops = list(nb.values())
        self.ops = []
        self.last_w = {}
        self.readers = {}
        if SEM_RESET:
            self.cnt = {e: 0 for e in ENGS}
            for pl in self.dma_sems.values():
                for sl in pl:
                    sl[1] = 0
                    sl[2] = None
            self.all_dma = []
        self.nphase += 1


_UID = [0]


def _uid():
    _UID[0] += 1
    return _UID[0]


def MM(P, out, lhsT, rhs, start, stop, r, w):
    return P.op('pe', lambda e: e.matmul(out, lhsT=lhsT, rhs=rhs, start=start, stop=stop), r, w)


def TR(P, out, in_, ident, r, w):
    return P.op('pe', lambda e: e.transpose(out=out, in_=in_, identity=ident), r, w)


def ACT(P, out, in_, func, r, w, bias=None, scale=None, accum=None):
    kw = {}
    if bias is not None:
        kw['bias'] = bias
    if scale is not None:
        kw['scale'] = scale
    if accum is not None:
        kw['accum_out'] = accum
    return P.op('act', lambda e: e.activation(out=out, in_=in_, func=func, **kw), r, w)


def TT(P, eng, out, in0, in1, op, r, w):
    return P.op(eng, lambda e: e.tensor_tensor(out=out, in0=in0, in1=in1, op=op), r, w)


def TS(P, eng, out, in0, s1, op0, r, w, s2=None, op1=None):
    if op1 is None:
        return P.op(eng, lambda e: e.tensor_scalar(out=out, in0=in0, scalar1=s1, scalar2=None, op0=op0), r, w)
    return P.op(eng, lambda e: e.tensor_scalar(out=out, in0=in0, scalar1=s1, scalar2=s2, op0=op0, op1=op1), r, w)


def STT(P, eng, out, in0, scalar, in1, op0, op1, r, w):
    return P.op(eng, lambda e: e.scalar_tensor_tensor(out=out, in0=in0, scalar=scalar, in1=in1, op0=op0, op1=op1), r, w)


def CP(P, eng, out, in_, r, w):
    if eng == 'act':
        return P.op('act', lambda e: e.activation(out=out, in_=in_, func=AF.Copy), r, w)
    return P.op(eng, lambda e: e.tensor_copy(out=out, in_=in_), r, w)


def RECIP(P, out, in_, r, w):
    return P.op('dve', lambda e: e.reciprocal(out=out, in_=in_), r, w)


def MEMSET(P, eng, ap, val, w):
    return P.op(eng, lambda e: e.memset(ap, val), (), w)


def DMA(P, q, out, in_, r=(), w=()):
    return P.op(q, lambda e: e.dma_start(out=out, in_=in_), r, w, dma=True)


def build_program(seqs, dbg=False):
    ntok = sum(seqs)
    nc = bass.Bass("TRN2", target_bir_lowering=False)
    kin = "ExternalInput"

    def din(name, shape, dt=F32):
        return nc.dram_tensor(name, list(shape), dt, kind=kin).ap()

    dbg_names = []

    def dscr(name, shape, dt):
        if dbg:
            dbg_names.append(name)
            return nc.dram_tensor(name, list(shape), dt, kind="ExternalOutput").ap()
        return nc.dram_tensor(name, list(shape), dt, kind="Internal").ap()

    x_all = din("x_all", [ntok, D_MODEL])
    w_fm = din("w_fm", [NFM, 128, 1024])
    w_tm = din("w_tm", [128, 8 * NTM])
    w_att = din("w_att", [128, 8 * 1024])
    w_ssm = din("w_ssm", [128, 16 * 1024])
    w_out = din("w_out", [128, 8 * 1024])
    consts = din("consts", [128, 8 * 128])
    vecs = din("vecs", [128, 400])
    bvecs = din("bvecs", [128, 64 + 64 + 32 + 1024 + 1024])
    rope_c = din("rope_c", [128, SMAX])
    rope_s = din("rope_s", [128, SMAX])
    y_all = nc.dram_tensor("y_all", [ntok, D_MODEL], F32, kind="ExternalOutput").ap()

    wfm_b = dscr("wfm_b", [NFM, 128, 1024], BF16)
    wtm_b = dscr("wtm_b", [128, 8 * NTM], BF16)
    watt_b = dscr("watt_b", [128, 8 * 1024], BF16)
    wssm_b = dscr("wssm_b", [128, 16 * 1024], BF16)
    wout_b = dscr("wout_b", [128, 8 * 1024], BF16)
    S_ = max(seqs)
    qT_d = dscr("qT_d", [8, 128, S_], BF16)
    kTa_d = dscr("kTa_d", [2, 128, S_], BF16)
    kTb_d = dscr("kTb_d", [2, 128, S_], BF16)
    v_d = dscr("v_d", [S_, 512], BF16)
    sgT_d = dscr("sgT_d", [1024, S_], BF16)
    gT_d = dscr("gT_d", [16, 128, S_], BF16)
    sz_d = dscr("sz_d", [S_, 2048], BF16)
    xs_d = dscr("xs_d", [S_, 2048], BF16)
    Btm_d = dscr("Btm_d", [S_, 512], BF16)
    BT_d = dscr("BT_d", [4, 128, S_], BF16)
    CT_d = dscr("CT_d", [4, 128, S_], BF16)
    dt_d = dscr("dt_d", [S_, 64], F32)
    gatt_d = dscr("gatt_d", [8, 128, S_], F32)
    yb_d = dscr("yb_d", [S_, 2048], F32)
    bnc_d = dscr("bnc_d", [4, 64 * 128], BF16)
    xres = None

    with ExitStack() as es0:
        P = Prog(nc, es0)

        with ExitStack() as es:
            sb = lambda n, s, d=F32: es.enter_context(nc.sbuf_tensor(n, s, d))
            NB0 = 4096
            f_in = [sb("p0f%d" % i, [128, NB0]) for i in range(2)]
            b_out = [sb("p0b%d" % i, [128, NB0], BF16) for i in range(2)]
            vec0 = sb("p0vec", [128, 400])
            DMA(P, LDQ, vec0[:], vecs[:, :], (), ['vec0'])
            cnt = [0]
            engs = ['dve', 'act', 'dve']

            def cast_piece(src, dst, n, scale_cols=None):
                i = cnt[0] % 2
                cnt[0] += 1
                DMA(P, LDQ, f_in[i][:, 0:n], src, (), [('f', i)])
                if scale_cols is None:
                    CP(P, engs[cnt[0] % 3], b_out[i][:, 0:n], f_in[i][:, 0:n], [('f', i)], [('b', i)])
                else:
                    o = None
                    for j, sc in enumerate(scale_cols):
                        TS(P, 'dve', b_out[i][:, j * 1024:(j + 1) * 1024], f_in[i][:, j * 1024:(j + 1) * 1024],
                           vec0[:, sc:sc + 1], ALU.mult, [('f', i), 'vec0'], [('b', i, j)])
                if scale_cols is None:
                    DMA(P, STQ, dst, b_out[i][:, 0:n], [('b', i)], ())
                else:
                    DMA(P, STQ, dst, b_out[i][:, 0:n], [('b', i, j) for j in range(len(scale_cols))], [('b', i)])

            for c in range(NFM // 4):
                i = cnt[0] % 2
                cnt[0] += 1
                DMA(P, LDQ, f_in[i][:].rearrange("p (c n) -> p c n", c=4), w_fm[c * 4:(c + 1) * 4].rearrange("c p n -> p c n"),
                    (), [('f', i)])
                CP(P, engs[cnt[0] % 3], b_out[i][:], f_in[i][:], [('f', i)], [('b', i)])
                DMA(P, STQ, wfm_b[c * 4:(c + 1) * 4].rearrange("c p n -> p c n"), b_out[i][:].rearrange("p (c n) -> p c n", c=4),
                    [('b', i)], ())
            tot = 8 * NTM
            o = 0
            while o < tot:
                n = min(NB0, tot - o)
                cast_piece(w_tm[:, o:o + n], wtm_b[:, o:o + n], n)
                o += n
            for o in range(0, 8192, NB0):
                cast_piece(w_att[:, o:o + NB0], watt_b[:, o:o + NB0], NB0)
            for o in range(0, 16384, NB0):
                cast_piece(w_ssm[:, o:o + NB0], wssm_b[:, o:o + NB0], NB0,
                           scale_cols=[VEC_SNW + o // 1024 + j for j in range(4)])
            for o in range(0, 8192, NB0):
                cast_piece(w_out[:, o:o + NB0], wout_b[:, o:o + NB0], NB0)
            P.emit_phase()

        tok0 = 0
        for si, S in enumerate(seqs):
            if "1" in PHASES:
                emit_phase1(nc, P, S, tok0, locals())
            if "2" in PHASES:
                emit_phase2(nc, P, S, tok0, locals())
            if "3" in PHASES:
                emit_phase3(nc, P, S, tok0, locals(), dirn=1)
                emit_phase3(nc, P, S, tok0, locals(), dirn=0)
            if "4" in PHASES:
                emit_phase4(nc, P, S, tok0, locals(), final=False)
            tok0 += S
        P.emit_phase(final=True)
    return nc, dbg_names


VEC_QW = 0
VEC_KW = 1
VEC_CW = 2
VEC_CB = 122
VEC_BG = 146
VEC_SNW = 162
VEC_END = 178
BV_DTB = 0
BV_ALOG = 64
BV_DSK = 128
BV_LNG = 160
BV_LNB = 160 + 1024

FM_KINDS = ([('q', i) for i in range(8)] + [('ka', i) for i in range(2)] + [('kb', i) for i in range(2)]
            + [('g', i) for i in range(8)] + [('gate', i) for i in range(16)]
            + [('xs', i) for i in range(16)] + [('B', i) for i in range(4)] + [('C', i) for i in range(4)])


LDQ = 'act'
STQ = 'sync'


def emit_phase1(nc, P, S, tok0, G):
    x_all = G['x_all']
    NT = S // 128
    NB = S // 512
    with ExitStack() as es:
        _u = _uid()
        sb = lambda n, s, d=F32: es.enter_context(nc.sbuf_tensor("%s_u%d" % (n, _u), s, d))
        ps = lambda n, s, d=F32: es.enter_context(nc.psum_tensor("%s_u%d" % (n, _u), s, d))
        xT = sb("xT", [128, 8, S], BF16)
        xin = [sb("xin%d" % i, [128, 1024]) for i in range(2)]
        cst = sb("cst1", [128, 8 * 128])
        ident = cst[:, 0:128]
        cstb = sb("cst1b", [128, 8 * 128], BF16)
        identb = cstb[:, 0:128]
        permb = cstb[:, 128:256]
        bonesb = cstb[:, 256:384]
        vec = sb("vec1", [128, 400])
        bv = sb("bv1", [128, 64])
        epsb = sb("epsb", [128, 1])
        wfm = [sb("wfm%d" % i, [128, 8, 128], BF16) for i in range(4)]
        wtm = [sb("wtm%d" % i, [128, 8, 512], BF16) for i in range(2)]
        pre = [sb("pre%d" % i, [128, S + 4]) for i in range(2)]
        acc = sb("acc", [128, S])
        cvo = [sb("cvo%d" % i, [128, S], BF16) for i in range(2)]
        rc = [sb("rc%d" % i, [128, 512]) for i in range(3)]
        rs = [sb("rs%d" % i, [128, 512]) for i in range(3)]
        sqb = [sb("sqb%d" % i, [128, 512], BF16) for i in range(2)]
        rstd = [sb("rstd%d" % i, [128, 512]) for i in range(2)]
        qnb = [sb("qnb%d" % i, [128, 512], BF16) for i in range(2)]
        t1 = [sb("t1_%d" % i, [128, 512]) for i in range(2)]
        t2 = [sb("t2_%d" % i, [128, 512]) for i in range(2)]
        ost = [sb("ost%d" % i, [128, 512], BF16) for i in range(4)]
        vst = [sb("vst%d" % i, [128, 4, 128], BF16) for i in range(2)]
        dtt = [sb("dtt%d" % i, [128, 64]) for i in range(2)]
        tms = [sb("tms%d" % i, [128, 4, 128], BF16) for i in range(2)]
        pm = [ps("pm%d" % i, [128, 512]) for i in range(3)]
        pa = [ps("pa%d" % i, [128, 512]) for i in range(2)]
        pt = [ps("pt%d" % i, [128, 512]) for i in range(1)]
        ptb = [ps("ptb%d" % i, [128, 1024], BF16) for i in range(2)]

        DMA(P, LDQ, cst[:], G['consts'][:, :], (), ['cst'])
        DMA(P, LDQ, vec[:], G['vecs'][:, :], (), ['vec'])
        DMA(P, LDQ, bv[:], G['bvecs'][:, BV_DTB:BV_DTB + 64], (), ['bv'])
        CP(P, 'dve', cstb[:], cst[:], ['cst'], ['cstb'])
        MEMSET(P, 'pool', epsb[:], RMS_EPS, ['epsb'])
        for i in range(2):
            MEMSET(P, 'pool', vst[i][:], 1.0, [('vst', i)])
            MEMSET(P, 'pool', pre[i][:, 0:2], 0.0, [('pre', i, 'pad')])
            MEMSET(P, 'pool', pre[i][:, S + 2:S + 4], 0.0, [('pre', i, 'pad')])

        wtm_b = G['wtm_b'].rearrange("p (k n) -> p k n", k=8)
        groups = [('v', 0, 256)] + [('z', 256 + j * 512, 512) for j in range(4)] + [('dt', 2304, 64)]

        def load_wtm(gi):
            kind, c0, n = groups[gi]
            DMA(P, LDQ, wtm[gi % 2][:, :, 0:n], wtm_b[:, :, c0:c0 + n], (), [('wtm', gi % 2)])

        def load_wfm(c):
            DMA(P, LDQ, wfm[c % 4][:], G['wfm_b'][c].rearrange("p (k n) -> p k n", k=8), (), [('wfm', c % 4)])

        def load_x(t):
            DMA(P, LDQ, xin[t % 2][:], x_all[tok0 + t * 128: tok0 + (t + 1) * 128, :], (), [('xin', t % 2)])

        load_x(0)
        load_wtm(0)
        load_wtm(1)
        ev = 0
        trb = [(pt[0], ('pt', 0)), (pa[0], ('pa', 0)), (pa[1], ('pa', 1))]
        for t in range(NT):
            xi = t % 2
            if t + 1 < NT:
                load_x(t + 1)
            for half in range(2):
                bank, bkey = trb[ev % 3]
                for k in range(4):
                    kk = half * 4 + k
                    TR(P, bank[:, k * 128:(k + 1) * 128], xin[xi][:, kk * 128:(kk + 1) * 128], ident,
                       [('xin', xi), 'cst'], [bkey])
                eng = 'dve' if ev % 2 == 0 else 'act'
                ev += 1
                CP(P, eng, xT[:, half * 4:(half + 1) * 4, t * 128:(t + 1) * 128],
                   bank[:].rearrange("p (k n) -> p k n", k=4), [bkey], [('xT', t)])
        xT_keys = [('xT', t) for t in range(NT)]

        pmi = 0
        osti = 0
        load_wfm(0)
        load_wfm(1)
        for gi, (kind, c0, n) in enumerate(groups):
            wi = gi % 2
            if gi >= 1 and gi + 1 < len(groups):
                load_wtm(gi + 1)
            for t in range(NT):
                pb = pm[pmi % 3]
                pk = ('pm', pmi % 3)
                pmi += 1
                for k in range(8):
                    MM(P, pb[:, 0:n], xT[:, k, t * 128:(t + 1) * 128], wtm[wi][:, k, 0:n], k == 0, k == 7,
                       [('xT', t), ('wtm', wi)], [pk])
                rows = slice(t * 128, (t + 1) * 128)
                if kind == 'v':
                    vi = t % 2
                    CP(P, 'act', vst[vi][:, :, 0:64], pb[:, 0:256].rearrange("p (g d) -> p g d", g=4), [pk], [('vst', vi)])
                    DMA(P, STQ, G['v_d'][rows, :], vst[vi][:].rearrange("p g d -> p (g d)"), [('vst', vi)], ())
                elif kind == 'z':
                    oi = osti % 4
                    osti += 1
                    ACT(P, ost[oi][:], pb[:], AF.Silu, [pk], [('ost', oi)])
                    DMA(P, STQ, G['sz_d'][rows, c0 - 256:c0 - 256 + 512], ost[oi][:], [('ost', oi)], ())
                else:
                    di = t % 2
                    TT(P, 'dve', dtt[di][:], pb[:, 0:64], bv[:], ALU.add, [pk, 'bv'], [('dtt', di)])
                    ACT(P, dtt[di][:], dtt[di][:], AF.Exp, [('dtt', di)], [('dtt', di)])
                    ACT(P, dtt[di][:], dtt[di][:], AF.Ln, [('dtt', di)], [('dtt', di)], bias=1.0)
                    DMA(P, STQ, G['dt_d'][rows, :], dtt[di][:], [('dtt', di)], ())

        items = [(c, tb) for c in range(NFM) for tb in range(NB)]
        qk_items = [it for it in items if FM_KINDS[it[0]][0] in ('q', 'ka', 'kb')]
        qk_index = {it: n for n, it in enumerate(qk_items)}

        def load_rope(n):
            if n < len(qk_items):
                tb = qk_items[n][1]
                cols = slice(tb * 512, (tb + 1) * 512)
                DMA(P, LDQ, rc[n % 3][:], G['rope_c'][:, cols], (), [('rc', n % 3)])
                DMA(P, LDQ, rs[n % 3][:], G['rope_s'][:, cols], (), [('rs', n % 3)])

        load_rope(0)
        load_rope(1)
        load_rope(2)
        state = {'pmi': pmi, 'osti': osti, 'tmi': 0}
        pend_g2 = []
        pend_g3 = []
        pend_tr = []
        pend_silu = []

        def qk_g2(n, pk, pb, kind):
            j = n % 2
            wcol = VEC_QW if kind == 'q' else VEC_KW
            MM(P, pa[0][:], bonesb, sqb[j][:], True, True, [('sqb', j), 'cstb'], [('pa', 0)])
            ACT(P, rstd[j][:], pa[0][:], AF.Ln, [('pa', 0), 'epsb'], [('rstd', j)], bias=epsb[:, 0:1])
            ACT(P, rstd[j][:], rstd[j][:], AF.Exp, [('rstd', j)], [('rstd', j)], scale=-0.5)
            STT(P, 'dve', qnb[j][:], pb[:], vec[:, wcol:wcol + 1], rstd[j][:], ALU.mult, ALU.mult,
                [pk, 'vec', ('rstd', j)], [('qnb', j)])

        def qk_g3(n, kind, idx, tb):
            j = n % 2
            r3 = n % 3
            cols = slice(tb * 512, (tb + 1) * 512)
            dst = {'q': G['qT_d'], 'ka': G['kTa_d'], 'kb': G['kTb_d']}[kind]
            MM(P, pa[1][:], permb, qnb[j][:], True, True, [('qnb', j), 'cstb'], [('pa', 1)])
            TT(P, 'pool', t1[j][:], qnb[j][:], rc[r3][:], ALU.mult, [('qnb', j), ('rc', r3)], [('t1', j)])
            TT(P, 'dve', t2[j][:], pa[1][:], rs[r3][:], ALU.mult, [('pa', 1), ('rs', r3)], [('t2', j)])
            oi = state['osti'] % 4
            state['osti'] += 1
            TT(P, 'dve', ost[oi][:], t1[j][:], t2[j][:], ALU.add, [('t1', j), ('t2', j)], [('ost', oi)])
            DMA(P, STQ, dst[idx][:, cols], ost[oi][:], [('ost', oi)], ())
            load_rope(n + 3)

        def conv_tail(kind, idx, ci3):
            dst = G['xs_d'] if kind == 'xs' else G['Btm_d']
            for t4 in range(NT // 4):
                ti = state['tmi'] % 2
                state['tmi'] += 1
                for tt in range(4):
                    t = t4 * 4 + tt
                    TR(P, ptb[ti][:, tt * 128:(tt + 1) * 128], cvo[ci3][:, t * 128:(t + 1) * 128],
                       identb, [('cvo', ci3), 'cstb'], [('ptb', ti)])
                CP(P, 'act', tms[ti][:],
                   ptb[ti][:, 0:512].rearrange("p (t n) -> p t n", t=4), [('ptb', ti)], [('tms', ti)])
                DMA(P, STQ,
                    dst[t4 * 512:(t4 + 1) * 512, idx * 128:(idx + 1) * 128].rearrange("(t p) c -> p t c", p=128),
                    tms[ti][:], [('tms', ti)], ())

        nconv = 0
        hsplit = (S * 5 // 8) // 512 * 512
        for c, (kind, idx) in enumerate(FM_KINDS):
            wi = c % 4
            if c + 2 < NFM:
                load_wfm(c + 2)
            pri = c % 2
            for tb in range(NB):
                pmi_ = state['pmi']
                state['pmi'] += 1
                pb = pm[pmi_ % 3]
                pk = ('pm', pmi_ % 3)
                cols = slice(tb * 512, (tb + 1) * 512)
                for k in range(8):
                    MM(P, pb[:], wfm[wi][:, k, :], xT[:, k, cols], k == 0, k == 7,
                       [('wfm', wi)] + xT_keys[tb * 4:(tb + 1) * 4], [pk])
                if pend_g3:
                    pend_g3.pop(0)()
                if pend_g2:
                    f2, f3 = pend_g2.pop(0)
                    f2()
                    pend_g3.append(f3)
                if tb == NB - 1 and len(pend_tr) > 1:
                    pend_tr.pop(0)()
                if kind in ('q', 'ka', 'kb'):
                    n = qk_index[(c, tb)]
                    j = n % 2
                    ACT(P, sqb[j][:], pb[:], AF.Square, [pk], [('sqb', j)])
                    pend_g2.append((lambda n=n, pk=pk, pb=pb, kind=kind: qk_g2(n, pk, pb, kind),
                                    lambda n=n, kind=kind, idx=idx, tb=tb: qk_g3(n, kind, idx, tb)))
                elif kind == 'g':
                    oi = state['osti'] % 4
                    state['osti'] += 1
                    ACT(P, ost[oi][:], pb[:], AF.Silu, [pk], [('ost', oi)])
                    DMA(P, STQ, G['sgT_d'][idx * 128:(idx + 1) * 128, cols], ost[oi][:], [('ost', oi)], ())
                elif kind == 'gate':
                    oi = state['osti'] % 4
                    state['osti'] += 1
                    ACT(P, ost[oi][:], pb[:], AF.Sigmoid, [pk, 'vec'], [('ost', oi)],
                        bias=vec[:, VEC_BG + idx:VEC_BG + idx + 1])
                    DMA(P, STQ, G['gT_d'][idx][:, cols], ost[oi][:], [('ost', oi)], ())
                else:
                    CP(P, 'act', pre[pri][:, 2 + tb * 512: 2 + (tb + 1) * 512], pb[:], [pk], [('pre', pri, tb)])
            while pend_silu:
                pend_silu.pop(0)()
            if kind in ('xs', 'B', 'C'):
                cc = {'xs': 0, 'B': 16, 'C': 20}[kind] + idx
                ci3 = nconv % 2
                nconv += 1
                prk = [('pre', pri, tb) for tb in range(NB)] + [('pre', pri, 'pad')]
                w0 = VEC_CW + cc * 5
                ACT(P, acc[:], pre[pri][:, 0:S], AF.Identity, prk + ['vec'], ['acc_a'], scale=vec[:, w0:w0 + 1])
                for j in range(1, 5):
                    STT(P, 'dve', acc[:], pre[pri][:, j:S + j], vec[:, w0 + j:w0 + j + 1], acc[:],
                        ALU.mult, ALU.add, prk + ['vec', 'acc_a'], ['acc_a'])

                def silu_store(kind=kind, idx=idx, ci3=ci3, cc=cc):
                    ACT(P, cvo[ci3][:], acc[:], AF.Silu, ['acc_a', 'vec'], [('cvo', ci3)],
                        bias=vec[:, VEC_CB + cc:VEC_CB + cc + 1])
                    if kind == 'B':
                        DMA(P, STQ, G['BT_d'][idx][:, 0:S], cvo[ci3][:], [('cvo', ci3)], ())
                    if kind == 'C':
                        DMA(P, STQ, G['CT_d'][idx][:, 0:S], cvo[ci3][:], [('cvo', ci3)], ())

                pend_silu.append(silu_store)
                if kind in ('xs', 'B'):
                    pend_tr.append(lambda kind=kind, idx=idx, ci3=ci3: conv_tail(kind, idx, ci3))
                else:
                    pend_tr.append(lambda: None)
        while pend_g3 or pend_g2:
            if pend_g3:
                pend_g3.pop(0)()
            if pend_g2:
                f2, f3 = pend_g2.pop(0)
                f2()
                pend_g3.append(f3)
        while pend_silu:
            pend_silu.pop(0)()
        while pend_tr:
            pend_tr.pop(0)()
        P.emit_phase()


def emit_phase2(nc, P, S, tok0, G):
    NKB = S // 128
    NQB = S // 512
    with ExitStack() as es:
        _u = _uid()
        sb = lambda n, s, d=F32: es.enter_context(nc.sbuf_tensor("%s_u%d" % (n, _u), s, d))
        ps = lambda n, s, d=F32: es.enter_context(nc.psum_tensor("%s_u%d" % (n, _u), s, d))
        kTa = sb("kTa", [128, 2, S], BF16)
        kTb = sb("kTb", [128, 2, S], BF16)
        vv = sb("vv", [128, NKB, 512], BF16)
        watt = sb("watt", [128, 8, 1024], BF16)
        qT = [sb("qT%d" % i, [128, 8, 512], BF16) for i in range(2)]
        sg2 = [sb("sg2_%d" % i, [64, 16, 512], BF16) for i in range(2)]
        ga = [sb("ga%d" % i, [128, 8, 512], BF16) for i in range(2)]
        PT = [sb("PT%d" % i, [128, 1024], BF16) for i in range(3)]
        attT = [sb("attT%d" % i, [128, 8, 512], BF16) for i in range(2)]
        rden = [sb("rden%d" % i, [64, 512]) for i in range(2)]
        tmp = [sb("tmpa%d" % i, [64, 512]) for i in range(2)]
        gst = [sb("gst%d" % i, [128, 512]) for i in range(2)]
        st = [ps("st%d" % i, [128, 1024]) for i in range(3)]
        acc = [ps("acc%d" % i, [128, 512]) for i in range(2)]
        accs = [sb("accs%d" % i, [128, 512]) for i in range(2)]

        def load_resident():
            DMA(P, LDQ, kTa[:, :, :], G['kTa_d'][:, :, 0:S].rearrange("c p s -> p c s"), (), ['kTa'])
            DMA(P, LDQ, kTb[:, :, :], G['kTb_d'][:, :, 0:S].rearrange("c p s -> p c s"), (), ['kTb'])
            for t4 in range(0, NKB, 8):
                DMA(P, LDQ, vv[:, t4:t4 + 8, :], G['v_d'][t4 * 128:(t4 + 8) * 128, :].rearrange("(t p) c -> p t c", p=128),
                    (), [('vv', t4 // 8)])
            DMA(P, LDQ, watt[:], G['watt_b'].rearrange("p (k n) -> p k n", k=8), (), ['watt'])

        def load_q(qb):
            qi = qb % 2
            cols = slice(qb * 512, (qb + 1) * 512)
            DMA(P, LDQ, qT[qi][:], G['qT_d'][:, :, cols].rearrange("c p s -> p c s"), (), [('qT', qi)])
            DMA(P, LDQ, sg2[qi][:], G['sgT_d'][:, cols].rearrange("(h d) s -> d h s", d=64), (), [('sg2', qi)])
            DMA(P, LDQ, ga[qi][:], G['gT_d'][0:8, :, cols].rearrange("c p s -> p c s"), (), [('ga', qi)])

        units = [(qb, c, kb) for qb in range(NQB) for c in range(8) for kb in range(NKB)]

        def qk(u, ui):
            qb, c, kb = u
            qi = qb % 2
            g = c // 2
            kE, kEk = (kTa, 'kTa') if g % 2 == 0 else (kTb, 'kTb')
            kO, kOk = (kTb, 'kTb') if g % 2 == 0 else (kTa, 'kTa')
            si = ui % 3
            MM(P, st[si][:, 0:512], kE[0:64, g // 2, kb * 128:(kb + 1) * 128], qT[qi][0:64, c, :], True, True,
               [kEk, ('qT', qi)], [('st', si)])
            MM(P, st[si][:, 512:1024], kO[64:128, g // 2, kb * 128:(kb + 1) * 128], qT[qi][64:128, c, :], True, True,
               [kOk, ('qT', qi)], [('st', si)])

        def pv(u, ui):
            qb, c, kb = u
            qi = qb % 2
            g = c // 2
            si = ui % 3
            pi = ui % 3
            ACT(P, PT[pi][:], st[si][:], AF.Exp, [('st', si)], [('PT', pi)], scale=0.125)
            for par in range(2):
                ai = par
                MM(P, acc[ai][:], vv[:, kb, g * 128:(g + 1) * 128], PT[pi][:, par * 512:(par + 1) * 512],
                   kb == 0, kb == NKB - 1, [('vv', kb // 8), ('PT', pi)], [('acc', ai)])
            if kb == NKB - 1:
                for par in range(2):
                    CP(P, 'dve', accs[par][:], acc[par][:], [('acc', par)], [('accs', par)])
                for par in range(2):
                    ai = par
                    h = 2 * c + par
                    ri = par
                    RECIP(P, rden[ri][:], accs[ai][64:128, :], [('accs', ai)], [('rden', ri)])
                    TT(P, 'dve', tmp[ri][:], accs[ai][0:64, :], rden[ri][:], ALU.mult, [('accs', ai), ('rden', ri)], [('tmp', ri)])
                    TT(P, 'pool', attT[qi][par * 64:(par + 1) * 64, c, :], tmp[ri][:], sg2[qi][:, h, :], ALU.mult,
                       [('tmp', ri), ('sg2', qi)], [('attT', qi, h)])
                if c == 7:
                    cols = slice(qb * 512, (qb + 1) * 512)
                    for ec in range(8):
                        p2 = ec % 2
                        ppb = st[si][:, p2 * 512:(p2 + 1) * 512]
                        for kc in range(8):
                            MM(P, ppb, watt[:, kc, ec * 128:(ec + 1) * 128], attT[qi][:, kc, :], kc == 0, kc == 7,
                               ['watt'] + [('attT', qi, hh) for hh in range(16)], [('st', si)])
                        TT(P, 'dve', gst[p2][:], ppb, ga[qi][:, ec, :], ALU.mult, [('st', si), ('ga', qi)], [('gst', p2)])
                        DMA(P, STQ, G['gatt_d'][ec][:, cols], gst[p2][:], [('gst', p2)], ())

        load_q(0)
        load_resident()
        if NQB > 1:
            load_q(1)
        qk(units[0], 0)
        qk(units[1], 1)
        for ui, u in enumerate(units):
            if u[1] == 0 and u[2] == 0 and u[0] >= 1 and u[0] + 1 < NQB:
                load_q(u[0] + 1)
            if ui + 2 < len(units):
                qk(units[ui + 2], ui + 2)
            pv(u, ui)
        P.emit_phase()


def emit_phase3(nc, P, S, tok0, G, dirn):
    NT = S // 128
    order = list(range(NT)) if dirn == 0 else list(range(NT - 1, -1, -1))
    with ExitStack() as es:
        _u = _uid()
        sb = lambda n, s, d=F32: es.enter_context(nc.sbuf_tensor("%s_u%d" % (n, _u), s, d))
        ps = lambda n, s, d=F32: es.enter_context(nc.psum_tensor("%s_u%d" % (n, _u), s, d))
        cst = sb("cst3", [128, 8 * 128])
        cstb = sb("cst3b", [128, 8 * 128], BF16)
        tri = cst[:, (3 + dirn) * 128:(4 + dirn) * 128]
        ntri = cst[:, (6 + dirn) * 128:(7 + dirn) * 128]
        ones = cst[:, 5 * 128:6 * 128]
        maskb = cstb[:, (3 + dirn) * 128:(4 + dirn) * 128]
        bv = sb("bv3", [128, 96])
        aneg = sb("aneg", [128, 64])
        negm = sb("negm", [128, 4, 128], BF16)
        L66 = [sb("L66_%d" % i, [66, 128], BF16) for i in range(2)]
        R66 = [sb("R66_%d" % i, [66, 4096], BF16) for i in range(2)]
        xs_t = [sb("xs_t%d" % i, [128, 32, 64], BF16) for i in range(3)]
        Btm_t = [sb("Btm_t%d" % i, [128, 512], BF16) for i in range(3)]
        BT_t = [sb("BT_t%d" % i, [128, 4, 128], BF16) for i in range(3)]
        CT_t = [sb("CT_t%d" % i, [128, 4, 128], BF16) for i in range(3)]
        dt_t = [sb("dt_t%d" % i, [128, 64]) for i in range(3)]
        adt = [sb("adt%d" % i, [128, 32]) for i in range(2)]
        eacs = [sb("eacs%d" % i, [128, 32]) for i in range(2)]
        wd = [sb("wd%d" % i, [128, 32]) for i in range(2)]
        dec = [sb("dec%d" % i, [128, 32]) for i in range(2)]
        dw = [sb("dw%d" % i, [128, 32]) for i in range(2)]
        lndt = [sb("lndt%d" % i, [128, 32]) for i in range(2)]
        Lb = [sb("Lb%d" % i, [64, 128], BF16) for i in range(2)]
        xw = [sb("xw%d" % i, [128, 32, 64], BF16) for i in range(2)]
        cbm = [sb("cbm%d" % i, [128, 4, 128], BF16) for i in range(2)]
        Lm = [sb("Lm%d" % i, [128, 4, 128], BF16) for i in range(2)]
        sc = [sb("sc%d" % i, [128, 4, 128], BF16) for i in range(2)]
        yo = [sb("yo%d" % i, [128, 8, 64]) for i in range(2)]
        ydir = [sb("ydir%d" % i, [128, 2048]) for i in range(2)]
        hf = sb("hf", [128, 32, 64])
        hb = sb("hb", [128, 2048], BF16)
        if dirn == 0:
            yb_t = [sb("yb_t%d" % i, [128, 2048]) for i in range(3)]
            Dd = sb("Dd", [128, 32, 128], BF16)
        pss = ps("pss", [128, 512])
        pcb = ps("pcb", [128, 512])
        seg = [ps("seg%d" % i, [128, 512]) for i in range(2)]
        pyd2 = [ps("pyd%d" % i, [128, 512]) for i in range(2)]
        pyo = ps("pyo", [128, 512])
        pst = ps("pst", [128, 512])

        DMA(P, LDQ, cst[:], G['consts'][:, :], (), ['cst'])
        DMA(P, LDQ, bv[:], G['bvecs'][:, BV_ALOG:BV_ALOG + 96], (), ['bv'])
        CP(P, 'dve', cstb[:], cst[:], ['cst'], ['cstb'])
        TS(P, 'dve', negm[:], tri.unsqueeze(1).to_broadcast([128, 4, 128]), -1.0, ALU.add, ['cst'], ['negm'],
           s2=30000.0, op1=ALU.mult)
        ACT(P, aneg[:], bv[:, 0:64], AF.Exp, ['bv'], ['aneg'])
        TS(P, 'dve', aneg[:], aneg[:], -1.0, ALU.mult, ['aneg'], ['aneg'])
        for i in range(2):
            MEMSET(P, 'pool', L66[i][64:66, :], -1.0, [('L66c', i)])
            for hh in range(2):
                CP(P, 'dve', R66[i][hh * 32:(hh + 1) * 32, :].rearrange("p (h i) -> p h i", h=32),
                   cstb[hh * 32:(hh + 1) * 32, hh * 32:(hh + 1) * 32].unsqueeze(2).to_broadcast([32, 32, 128]),
                   ['cstb'], [('R66c', i, hh)])
        if dirn == 0:
            TT(P, 'dve', Dd[:], cstb[:, 0:128].unsqueeze(1).to_broadcast([128, 32, 128]),
               bv[:, 64:96].unsqueeze(2).to_broadcast([128, 32, 128]), ALU.mult, ['cstb', 'bv'], ['Dd'])
        MEMSET(P, 'pool', hf[:], 0.0, [('hf', g) for g in range(4)])
        MEMSET(P, 'pool', hb[:], 0.0, [('hb', g) for g in range(4)])
        dcol = dirn * 32

        def loads(ci):
            t = order[ci]
            l = ci % 3
            rows = slice(t * 128, (t + 1) * 128)
            cols = slice(t * 128, (t + 1) * 128)
            DMA(P, LDQ, xs_t[l][:].rearrange("p h d -> p (h d)"), G['xs_d'][rows, :], (), [('xs_t', l)])
            DMA(P, LDQ, Btm_t[l][:], G['Btm_d'][rows, :], (), [('Btm_t', l)])
            DMA(P, LDQ, BT_t[l][:], G['BT_d'][:, :, cols].rearrange("g p s -> p g s"), (), [('BT_t', l)])
            DMA(P, LDQ, CT_t[l][:], G['CT_d'][:, :, cols].rearrange("g p s -> p g s"), (), [('CT_t', l)])
            DMA(P, LDQ, dt_t[l][:], G['dt_d'][rows, :], (), [('dt_t', l)])
            if dirn == 0:
                DMA(P, LDQ, yb_t[l][:], G['yb_d'][rows, :], [('ybd', t)], [('yb_t', l)])

        def prologue(ci):
            t = order[ci]
            i = ci % 2
            l = ci % 3
            TT(P, 'dve', adt[i][:], dt_t[l][:, dcol:dcol + 32], aneg[:, dcol:dcol + 32], ALU.mult,
               [('dt_t', l), 'aneg'], [('adt', i)])
            ACT(P, lndt[i][:], dt_t[l][:, dcol:dcol + 32], AF.Ln, [('dt_t', l)], [('lndt', i)])
            ACT(P, lndt[i][:], lndt[i][:], AF.Identity, [('lndt', i)], [('lndt', i)], scale=-1.0)
            MM(P, pss[0:32, 0:128], adt[i][:], tri, True, True, [('adt', i), 'cst'], ['pss'])
            MM(P, pss[0:32, 224:352], adt[i][:], tri, True, False, [('adt', i), 'cst'], ['pss'])
            MM(P, pss[0:32, 224:352], lndt[i][:], cst[:, 0:128], False, True, [('lndt', i), 'cst'], ['pss'])
            MM(P, pss[:, 128:160], tri, adt[i][:], True, True, [('adt', i), 'cst'], ['pss'])
            MM(P, pss[:, 160:192], ones, adt[i][:], True, True, [('adt', i), 'cst'], ['pss'])
            MM(P, pss[:, 192:224], ntri, adt[i][:], True, True, [('adt', i), 'cst'], ['pss'])
            ACT(P, Lb[i][0:32, :], pss[0:32, 0:128], AF.Identity, ['pss'], [('Lba', i)], scale=-1.0)
            STT(P, 'dve', Lb[i][32:64, :], pss[0:32, 0:128], -1.0, Lb[i][0:32, :], ALU.mult, ALU.subtract,
                ['pss', ('Lba', i)], [('Lbb', i)])
            ACT(P, L66[i][0:32, :], pss[0:32, 224:352], AF.Identity, ['pss'], [('L66a', i)], scale=-1.0)
            STT(P, 'dve', L66[i][32:64, :], pss[0:32, 224:352], -1.0, L66[i][0:32, :], ALU.mult, ALU.subtract,
                ['pss', ('L66a', i)], [('L66b', i)])
            ACT(P, eacs[i][:], pss[:, 128:160], AF.Exp, ['pss'], [('eacs', i)])
            ACT(P, dec[i][:], pss[:, 160:192], AF.Exp, ['pss'], [('dec', i)])
            ACT(P, wd[i][:], pss[:, 192:224], AF.Exp, ['pss'], [('wd', i)])
            TT(P, 'dve', dw[i][:], dt_t[l][:, dcol:dcol + 32], wd[i][:], ALU.mult, [('dt_t', l), ('wd', i)], [('dw', i)])
            DMA(P, STQ, G['bnc_d'][i].rearrange("(p n) -> p n", p=64), Lb[i][0:64, :],
                [('Lba', i), ('Lbb', i)], [('bnc', i)])
            DMA(P, STQ, R66[i][64:66, :], G['bnc_d'][i].rearrange("(p n) -> p n", p=2), [('bnc', i)], [('R66', i)])
            TT(P, 'pool', xw[i][:], xs_t[l][:], dw[i][:].unsqueeze(2).to_broadcast([128, 32, 64]), ALU.mult,
               [('xs_t', l), ('dw', i)], [('xw', i)])
            for g in range(4):
                MM(P, pcb[:, g * 128:(g + 1) * 128], BT_t[l][:, g, :], CT_t[l][:, g, :], True, True,
                   [('BT_t', l), ('CT_t', l)], ['pcb'])
            CP(P, 'act', cbm[i][:], pcb[:].rearrange("p (g n) -> p g n", g=4), ['pcb'], [('cbm', i)])

        def seg_mm(u):
            ci, q8 = u // 8, u % 8
            i = ci % 2
            si = u % 2
            MM(P, seg[si][:], L66[i][0:66, :], R66[i][0:66, q8 * 512:(q8 + 1) * 512], True, False,
               [('L66a', i), ('L66b', i), ('L66c', i), ('R66', i), ('R66c', i, 0), ('R66c', i, 1)], [('seg', si)])
            MM(P, seg[si][:], cstb[:, 0:128], negm[:].rearrange("p h n -> p (h n)"), False, True,
               ['cstb', 'negm'], [('seg', si)])

        def unit(u):
            ci, q8 = u // 8, u % 8
            t = order[ci]
            i = ci % 2
            l = ci % 3
            g = q8 // 2
            si = u % 2
            li = u % 2
            pyd = pyd2[g % 2]
            pydk = ('pyd', g % 2)
            ACT(P, Lm[li][:].rearrange("p h n -> p (h n)"), seg[si][:], AF.Exp, [('seg', si)], [('Lm', li)])
            if q8 % 2 == 1:
                while pend_hb:
                    pend_hb.pop(0)()
            TT(P, 'dve', sc[li][:], Lm[li][:], cbm[i][:, g:g + 1, :].to_broadcast([128, 4, 128]), ALU.mult,
               [('Lm', li), ('cbm', i)], [('sc', li)])
            for hh in range(4):
                h = q8 * 4 + hh
                c0 = (q8 % 2) * 256 + hh * 64
                MM(P, pyd[:, c0:c0 + 64], sc[li][:, hh, :], xs_t[l][:, h, :], True, dirn == 1,
                   [('sc', li), ('xs_t', l)], [pydk])
                if dirn == 0:
                    MM(P, pyd[:, c0:c0 + 64], Dd[:, h, :], xs_t[l][:, h, :], False, True,
                       ['Dd', ('xs_t', l)], [pydk])
            while pend_e:
                pend_e.pop(0)()
            if q8 % 2 == 1:
                yi = g % 2
                MM(P, pyo[:], CT_t[l][:, g, :], hb[:, g * 512:(g + 1) * 512], True, True,
                   [('CT_t', l), ('hb', g)], ['pyo'])
                MM(P, pst[:], Btm_t[l][:, g * 128:(g + 1) * 128], xw[i][:, g * 8:(g + 1) * 8, :].rearrange("p h d -> p (h d)"),
                   True, True, [('Btm_t', l), ('xw', i)], ['pst'])

                def group_end(g=g, i=i, l=l, t=t, yi=yi, pyd=pyd, pydk=pydk):
                    TT(P, 'pool', hf[:, g * 8:(g + 1) * 8, :], hf[:, g * 8:(g + 1) * 8, :],
                       dec[i][:, g * 8:(g + 1) * 8].unsqueeze(2).to_broadcast([128, 8, 64]), ALU.mult,
                       [('hf', g), ('dec', i)], [('hf', g)])
                    TT(P, 'dve', yo[yi][:], pyo[:].rearrange("p (h d) -> p h d", h=8),
                       eacs[i][:, g * 8:(g + 1) * 8].unsqueeze(2).to_broadcast([128, 8, 64]), ALU.mult,
                       ['pyo', ('eacs', i)], [('yo', yi)])
                    TT(P, 'dve', ydir[i][:, g * 512:(g + 1) * 512], pyd[:], yo[yi][:].rearrange("p h d -> p (h d)"), ALU.add,
                       [pydk, ('yo', yi)], [('ydir', i, g)])
                    TT(P, 'dve', hf[:, g * 8:(g + 1) * 8, :], hf[:, g * 8:(g + 1) * 8, :],
                       pst[:].rearrange("p (h d) -> p h d", h=8), ALU.add, [('hf', g), 'pst'], [('hf', g)])
                    pend_hb.append(lambda: CP(P, 'pool', hb[:, g * 512:(g + 1) * 512],
                                              hf[:, g * 8:(g + 1) * 8, :].rearrange("p h d -> p (h d)"),
                                              [('hf', g)], [('hb', g)]))
                    if g == 3:
                        rows = slice(t * 128, (t + 1) * 128)
                        ykeys = [('ydir', i, gg) for gg in range(4)]
                        if dirn == 1:
                            DMA(P, STQ, G['yb_d'][rows, :], ydir[i][:], ykeys, ())
                        else:
                            TT(P, 'pool', ydir[i][:], ydir[i][:], yb_t[l][:], ALU.add, ykeys + [('yb_t', l)],
                               [('ysum', i)] + ykeys)
                            DMA(P, STQ, G['yb_d'][rows, :], ydir[i][:], [('ysum', i)] + ykeys, [('ybd', t)])

                pend_e.append(group_end)

        NU = NT * 8
        pend_hb = []
        pend_e = []
        loads(0)
        if NT > 1:
            loads(1)
        prologue(0)
        seg_mm(0)
        for u in range(NU):
            ci, q8 = u // 8, u % 8
            if q8 == 0:
                while pend_e:
                    pend_e.pop(0)()
                if ci + 2 < NT:
                    loads(ci + 2)
                if ci + 1 < NT:
                    prologue(ci + 1)
            if u + 1 < NU:
                seg_mm(u + 1)
            unit(u)
        while pend_e:
            pend_e.pop(0)()
        while pend_hb:
            pend_hb.pop(0)()
        P.emit_phase()


def emit_phase4(nc, P, S, tok0, G, final):
    NB = S // 512
    with ExitStack() as es:
        _u = _uid()
        sb = lambda n, s, d=F32: es.enter_context(nc.sbuf_tensor("%s_u%d" % (n, _u), s, d))
        ps = lambda n, s, d=F32: es.enter_context(nc.psum_tensor("%s_u%d" % (n, _u), s, d))
        cst = sb("cst4", [128, 128])
        identb = sb("cst4b", [128, 128], BF16)
        bv = sb("bv4", [128, 2048])
        wssm = sb("wssm", [128, 16, 1024], BF16)
        wout = sb("wout", [128, 8, 1024], BF16)
        y_t = [sb("y_t%d" % i, [128, 4, 512]) for i in range(2)]
        sz_t = [sb("sz_t%d" % i, [128, 4, 512], BF16) for i in range(2)]
        yz = [sb("yz%d" % i, [128, 4, 512]) for i in range(2)]
        junk = sb("junk4", [128, 1024], BF16)
        ss = [sb("ss%d" % i, [128, 4]) for i in range(2)]
        rs = [sb("rs4_%d" % i, [128, 4]) for i in range(2)]
        yn = [sb("yn%d" % i, [128, 4, 512], BF16) for i in range(4)]
        ynT = [sb("ynT%d" % i, [128, 16, 512], BF16) for i in range(2)]
        gs = [sb("gs%d" % i, [128, 8, 512], BF16) for i in range(1)]
        gat = [sb("gat%d" % i, [128, 8, 512]) for i in range(1)]
        m1 = [sb("m1_%d" % i, [128, 512]) for i in range(2)]
        mixT = [sb("mixT%d" % i, [128, 8, 512], BF16) for i in range(1)]
        xr = [sb("xr%d" % i, [128, 1024]) for i in range(2)]
        rr = [sb("rr%d" % i, [128, 1024]) for i in range(2)]
        st1 = [sb("st1_%d" % i, [128, 8]) for i in range(2)]
        lno = [sb("lno%d" % i, [128, 1024]) for i in range(2)]
        ptb = [ps("ptb4_%d" % i, [128, 1024], BF16) for i in range(2)]
        pso = [ps("pso%d" % i, [128, 512]) for i in range(2)]
        po = [ps("po%d" % i, [128, 512]) for i in range(2)]

        DMA(P, LDQ, cst[:], G['consts'][:, 0:128], (), ['cst'])
        DMA(P, LDQ, bv[:], G['bvecs'][:, BV_LNG:BV_LNG + 2048], (), ['bv'])
        CP(P, 'dve', identb[:], cst[:], ['cst'], ['identb'])
        cnt = [0]

        def front_elem(tb, tt):
            t = tb * 4 + tt
            i = t % 2
            rows = slice(t * 128, (t + 1) * 128)
            DMA(P, LDQ, y_t[i][:].rearrange("p g n -> p (g n)"), G['yb_d'][rows, :], (), [('y_t', i)])
            DMA(P, LDQ, sz_t[i][:].rearrange("p g n -> p (g n)"), G['sz_d'][rows, :], (), [('sz_t', i)])
            TT(P, 'dve', yz[i][:], y_t[i][:], sz_t[i][:], ALU.mult, [('y_t', i), ('sz_t', i)], [('yz', i)])
            for g in range(4):
                ACT(P, junk[:, 0:512], yz[i][:, g, :], AF.Square, [('yz', i)], ['junk', ('ss', i, g)],
                    accum=ss[i][:, g:g + 1])
            ACT(P, rs[i][:], ss[i][:], AF.Sqrt, [('ss', i, g) for g in range(4)], [('rs', i)],
                scale=1.0 / 512, bias=RMS_EPS)
            RECIP(P, rs[i][:], rs[i][:], [('rs', i)], [('rs', i)])
            TT(P, 'pool', yn[tt][:], yz[i][:], rs[i][:].unsqueeze(2).to_broadcast([128, 4, 512]), ALU.mult,
               [('yz', i), ('rs', i)], [('yn', tt)])

        def front_tr(tb, tt):
            bi = tb % 2
            t = tb * 4 + tt
            i = tt
            for k4 in range(4):
                pi = cnt[0] % 2
                cnt[0] += 1
                for k in range(4):
                    kc = k4 * 4 + k
                    TR(P, ptb[pi][:, k * 128:(k + 1) * 128], yn[i][:, kc // 4, (kc % 4) * 128:(kc % 4 + 1) * 128],
                       identb[:], [('yn', i), 'identb'], [('ptb', pi)])
                CP(P, 'act' if k4 % 2 else 'dve', ynT[bi][:, k4 * 4:(k4 + 1) * 4, tt * 128:(tt + 1) * 128],
                   ptb[pi][:, 0:512].rearrange("p (k n) -> p k n", k=4), [('ptb', pi)], [('ynT', bi, tt, k4)])

        def load_gates(tb):
            cols = slice(tb * 512, (tb + 1) * 512)
            DMA(P, LDQ, gs[0][:], G['gT_d'][8:16, :, cols].rearrange("c p s -> p c s"), (), [('gs', 0)])
            DMA(P, LDQ, gat[0][:], G['gatt_d'][:, :, cols].rearrange("c p s -> p c s"), (), [('gat', 0)])

        def back(tb):
            bi = tb % 2
            cols = slice(tb * 512, (tb + 1) * 512)
            ynk = [('ynT', bi, tt, k4) for tt in range(4) for k4 in range(4)]
            for ec in range(8):
                p2 = ec % 2
                for kc in range(16):
                    MM(P, pso[p2][:], wssm[:, kc, ec * 128:(ec + 1) * 128], ynT[bi][:, kc, :], kc == 0, kc == 15,
                       ['wssm'] + ynk, [('pso', p2)])
                TT(P, 'dve', m1[p2][:], pso[p2][:], gs[0][:, ec, :], ALU.mult, [('pso', p2), ('gs', 0)], [('m1', p2)])
                TT(P, 'pool', mixT[0][:, ec, :], m1[p2][:], gat[0][:, ec, :], ALU.add, [('m1', p2), ('gat', 0)],
                   [('mixT', 0, ec)])
                if ec % 2 == 1 and tb + 1 < NB:
                    front_elem(tb + 1, ec // 2)
            mk = [('mixT', 0, ec) for ec in range(8)]
            if tb + 1 < NB:
                load_gates(tb + 1)
                for tt in range(4):
                    front_tr(tb + 1, tt)
            for tt in range(4):
                t = tb * 4 + tt
                i = t % 2
                grow = slice(tok0 + t * 128, tok0 + (t + 1) * 128)
                DMA(P, LDQ, xr[i][:], G['x_all'][grow, :], (), [('xr', i)])
                for half in range(2):
                    for ke in range(8):
                        MM(P, po[half][:], mixT[0][:, ke, tt * 128:(tt + 1) * 128], wout[:, ke, half * 512:(half + 1) * 512],
                           ke == 0, ke == 7, ['wout'] + mk, [('po', half)])
                    STT(P, 'dve', rr[i][:, half * 512:(half + 1) * 512], xr[i][:, half * 512:(half + 1) * 512], ALPHA,
                        po[half][:], ALU.mult, ALU.add, [('xr', i), ('po', half)], [('rr', i, half)])
                rk = [('rr', i, 0), ('rr', i, 1)]
                ACT(P, junk[:], rr[i][:], AF.Identity, rk, ['junk', ('st1', i, 0)], accum=st1[i][:, 0:1])
                ACT(P, junk[:], rr[i][:], AF.Square, rk, ['junk', ('st1', i, 1)], accum=st1[i][:, 1:2])
                sk_ = [('st1', i, 0), ('st1', i, 1)]
                TS(P, 'dve', st1[i][:, 2:4], st1[i][:, 0:2], 1.0 / 1024, ALU.mult, sk_, [('st1', i, 2)])
                TT(P, 'dve', st1[i][:, 4:5], st1[i][:, 2:3], st1[i][:, 2:3], ALU.mult, [('st1', i, 2)], [('st1', i, 4)])
                TT(P, 'dve', st1[i][:, 5:6], st1[i][:, 3:4], st1[i][:, 4:5], ALU.subtract, [('st1', i, 2), ('st1', i, 4)],
                   [('st1', i, 5)])
                ACT(P, st1[i][:, 6:7], st1[i][:, 5:6], AF.Sqrt, [('st1', i, 5)], [('st1', i, 6)], bias=LN_EPS)
                RECIP(P, st1[i][:, 6:7], st1[i][:, 6:7], [('st1', i, 6)], [('st1', i, 6)])
                STT(P, 'dve', st1[i][:, 7:8], st1[i][:, 2:3], -1.0, st1[i][:, 6:7], ALU.mult, ALU.mult,
                    [('st1', i, 2), ('st1', i, 6)], [('st1', i, 7)])
                ACT(P, lno[i][:], rr[i][:], AF.Identity, rk + [('st1', i, 6), ('st1', i, 7)], [('lno', i)],
                    scale=st1[i][:, 6:7], bias=st1[i][:, 7:8])
                TT(P, 'dve', lno[i][:], lno[i][:], bv[:, 0:1024], ALU.mult, [('lno', i), 'bv'], [('lno', i)])
                TT(P, 'pool', lno[i][:], lno[i][:], bv[:, 1024:2048], ALU.add, [('lno', i), 'bv'], [('lno', i)])
                DMA(P, STQ, G['y_all'][grow, :], lno[i][:], [('lno', i)], ())

        for tt in range(4):
            front_elem(0, tt)
        DMA(P, LDQ, wssm[:], G['wssm_b'].rearrange("p (k n) -> p k n", k=16), (), ['wssm'])
        load_gates(0)
        DMA(P, LDQ, wout[:], G['wout_b'].rearrange("p (k n) -> p k n", k=8), (), ['wout'])
        for tt in range(4):
            front_tr(0, tt)
        for tb in range(NB):
            back(tb)
        P.emit_phase(final=final)


def _perm64():
    return np.array([2 * i for i in range(32)] + [2 * i + 1 for i in range(32)])


def _host_layouts(w_in, b_gate, q_norm_w, k_norm_w, conv_w, conv_b, dt_bias_fwd, dt_bias_bwd, a_log_fwd,
                  a_log_bwd, d_skip, ssm_norm_w, w_att_proj, w_ssm_proj, w_out, ln_g, ln_b):
    f = np.float32
    W = np.asarray(w_in[0], f)
    pm = _perm64()
    q_cols = np.concatenate([h * 64 + pm for h in range(16)])
    ka_cols = np.concatenate([1024 + g * 64 + pm for g in (0, 1, 2, 3)])
    kb_cols = np.concatenate([1024 + g * 64 + pm for g in (1, 0, 3, 2)])
    g_cols = 1536 + np.arange(1024)
    gate_cols = 7744 + np.arange(2048)
    xbc_cols = 4608 + np.arange(3072)
    fm_cols = np.concatenate([q_cols, ka_cols, kb_cols, g_cols, gate_cols, xbc_cols])
    assert fm_cols.shape[0] == NFM * 128
    wf = W[:, fm_cols].reshape(8, 128, NFM, 128).transpose(2, 1, 0, 3)
    w_fm = np.ascontiguousarray(wf).reshape(NFM, 128, 1024)
    tm_cols = np.concatenate([1280 + np.arange(256), 2560 + np.arange(2048), 7680 + np.arange(64)])
    wt = W[:, tm_cols].reshape(8, 128, NTM).transpose(1, 0, 2)
    w_tm = np.ascontiguousarray(wt).reshape(128, 8 * NTM)

    def kmajor(w, nk):
        n = w.shape[1]
        return np.ascontiguousarray(np.asarray(w, f).reshape(nk, 128, n).transpose(1, 0, 2)).reshape(128, nk * n)

    vecs = np.zeros((128, 400), f)
    vecs[:, VEC_QW] = np.asarray(q_norm_w[0], f)[np.tile(pm, 2)]
    vecs[:, VEC_KW] = np.asarray(k_norm_w[0], f)[np.tile(pm, 2)]
    cw = np.asarray(conv_w[0], f)
    vecs[:, VEC_CW:VEC_CW + 120] = cw.reshape(5, 24, 128).transpose(2, 1, 0).reshape(128, 120)
    vecs[:, VEC_CB:VEC_CB + 24] = np.asarray(conv_b[0], f).reshape(24, 128).T
    vecs[:, VEC_BG:VEC_BG + 16] = np.asarray(b_gate[0], f).reshape(16, 128).T
    vecs[:, VEC_SNW:VEC_SNW + 16] = np.asarray(ssm_norm_w[0], f).reshape(16, 128).T
    row = np.concatenate([np.asarray(dt_bias_fwd[0], f), np.asarray(dt_bias_bwd[0], f),
                          np.asarray(a_log_fwd[0], f), np.asarray(a_log_bwd[0], f),
                          np.asarray(d_skip[0], f), np.asarray(ln_g[0], f), np.asarray(ln_b[0], f)])
    bvecs = np.ascontiguousarray(np.tile(row[None, :], (128, 1)))
    ident = np.eye(128, dtype=f)
    perm = np.zeros((128, 128), f)
    for d in range(128):
        blk, dd = d // 64, d % 64
        perm[blk * 64 + (dd + 32) % 64, d] = 1.0
    bones = np.zeros((128, 128), f)
    bones[:64, :64] = 1.0 / 64
    bones[64:, 64:] = 1.0 / 64
    tt, ii = np.meshgrid(np.arange(128), np.arange(128), indexing='ij')
    tri_f = (tt <= ii).astype(f)
    tri_b = (tt >= ii).astype(f)
    ones = np.ones((128, 128), f)
    consts = np.ascontiguousarray(np.concatenate([ident, perm, bones, tri_f, tri_b, ones, 1 - tri_f, 1 - tri_b], axis=1))
    s = np.arange(SMAX)
    rowp = (s // GRID_W).astype(f)
    colp = (s % GRID_W).astype(f)
    freqs = (10000.0 ** (-np.arange(0, 32, 2, dtype=f) / f(32))).astype(f)
    ang = np.concatenate([rowp[:, None] * freqs, colp[:, None] * freqs], axis=-1).astype(f)
    cos = np.cos(ang).astype(f).T
    sin = np.sin(ang).astype(f).T
    rope_c = np.zeros((128, SMAX), f)
    rope_s = np.zeros((128, SMAX), f)
    for p in range(128):
        dd = p % 64
        i = dd % 32
        rope_c[p] = cos[i]
        rope_s[p] = -sin[i] if dd < 32 else sin[i]
    return dict(w_fm=w_fm, w_tm=w_tm, w_att=kmajor(w_att_proj[0], 8), w_ssm=kmajor(w_ssm_proj[0], 16),
                w_out=kmajor(w_out[0], 8), consts=consts, vecs=vecs, bvecs=bvecs, rope_c=rope_c, rope_s=rope_s)


_CACHE = {}


def kernel(x_prompt, x_sample, **params):
    xp = np.asarray(x_prompt, np.float32)
    xs = np.asarray(x_sample, np.float32)
    shared = _host_layouts(**params)
    bp, sp = xp.shape[0], xp.shape[1]
    bs, ss = xs.shape[0], xs.shape[1]
    npc, nsc = bp // N_CORES, bs // N_CORES
    seqs = [sp] * npc + [ss] * nsc
    if 'nc' not in _CACHE:
        _CACHE['nc'] = build_program(seqs)[0]
    nc = _CACHE['nc']
    in_maps = []
    for c in range(N_CORES):
        parts = [xp[c * npc + i] for i in range(npc)] + [xs[c * nsc + i] for i in range(nsc)]
        m = dict(shared)
        m['x_all'] = np.ascontiguousarray(np.concatenate(parts, axis=0))
        in_maps.append(m)
    res = run_bass_kernel_spmd(nc, in_maps, core_ids=list(range(N_CORES)))
    yp = np.empty_like(xp)
    ys = np.empty_like(xs)
    for c in range(N_CORES):
        y = res.results[c]["y_all"]
        o = 0
        for i in range(npc):
            yp[c * npc + i] = y[o:o + sp]
            o += sp
        for i in range(nsc):
            ys[c * nsc + i] = y[o:o + ss]
            o += ss
    return (yp, ys)
```
